# Optimizing a Trainium2 kernel written in Bass

```python
import math
import jax, jax.numpy as jnp
from jax import lax
import numpy as np

D_MODEL = 1024
BATCH = 16
SEQ = 2048
DEPTH = 4

N_POOL_GROUPS = 4
POOL_GROUP_DIM = 128
POOL_WIDTH = N_POOL_GROUPS * POOL_GROUP_DIM
POOL_WINDOWS = (2, 4, 8, 16)

ATT_HEADS = 4
ATT_QK_DIM = 64
ATT_V_DIM = 2 * ATT_QK_DIM
ATT_QK_WIDTH = ATT_HEADS * 2 * ATT_QK_DIM
ATT_WIDTH = ATT_HEADS * ATT_V_DIM
Q_BLOCK = 128

REC_HEADS = 4
REC_K_DIM = 128
REC_V_DIM = 128
REC_K_WIDTH = REC_HEADS * REC_K_DIM
REC_WIDTH = REC_HEADS * REC_V_DIM
REC_CHUNK = 64

N_BRANCHES = 3
FFN_HIDDEN = ((math.ceil(8 * D_MODEL / 3) + 255) // 256) * 256
NORM_EPS = 1e-6

IN_SIZES = (
    POOL_WIDTH,
    ATT_QK_WIDTH, ATT_QK_WIDTH, ATT_WIDTH,
    REC_K_WIDTH, REC_K_WIDTH, REC_K_WIDTH,
    REC_WIDTH, REC_WIDTH,
    N_BRANCHES * D_MODEL,
)
IN_COLS = (POOL_WIDTH + 2 * ATT_QK_WIDTH + ATT_WIDTH + 3 * REC_K_WIDTH
           + 2 * REC_WIDTH + N_BRANCHES * D_MODEL)

kernel_name = "hybrid_pool_diffattn_hgrn2_encoder"


def rms_norm(x, g):
    xf = x.astype(jnp.float32)
    y = xf * lax.rsqrt(jnp.mean(xf * xf, axis=-1, keepdims=True) + NORM_EPS)
    return (y * g).astype(x.dtype)


def split_cols(proj):
    out, off = [], 0
    for n in IN_SIZES:
        out.append(proj[..., off:off + n])
        off += n
    return out


def pool_mixer(u, pool_w, pool_scale):
    B, S, _ = u.shape
    uf = u.astype(jnp.float32).reshape(B, S, N_POOL_GROUPS, POOL_GROUP_DIM)
    cs = jnp.concatenate([jnp.zeros((B, 1, N_POOL_GROUPS, POOL_GROUP_DIM), jnp.float32),
                          jnp.cumsum(uf, axis=1)], axis=1)
    t = jnp.arange(S)
    outs = []
    for g, w in enumerate(POOL_WINDOWS):
        lo = jnp.clip(t - w // 2, 0, S - 1)
        hi = jnp.clip(t + w // 2 - 1, 0, S - 1)
        cnt = (hi - lo + 1).astype(jnp.float32)
        mean = (cs[:, hi + 1, g] - cs[:, lo, g]) / cnt[None, :, None]
        outs.append(mean - uf[:, :, g])
    d = jnp.stack(outs, axis=2).astype(u.dtype)
    y = jnp.einsum('bsgc,gcd->bsgd', d, pool_w).reshape(B, S, POOL_WIDTH)
    return y * pool_scale


def diff_attention(q, k, v, lam, gain, lambda_init):
    B, S, _ = q.shape
    q = q.reshape(B, S, ATT_HEADS, 2, ATT_QK_DIM)
    k = k.reshape(B, S, ATT_HEADS, 2, ATT_QK_DIM)
    v = v.reshape(B, S, ATT_HEADS, ATT_V_DIM)
    scale = ATT_QK_DIM ** -0.5
    slopes = jnp.exp2(-8.0 * jnp.arange(1, ATT_HEADS + 1, dtype=jnp.float32) / ATT_HEADS)
    pos = jnp.arange(S, dtype=jnp.float32)
    nb = S // Q_BLOCK
    qb = q.reshape(B, nb, Q_BLOCK, ATT_HEADS, 2, ATT_QK_DIM).swapaxes(0, 1)
    pb = pos.reshape(nb, Q_BLOCK)

    def block(args):
        qblk, qpos = args
        s = jnp.einsum('bqhcd,bkhcd->bhcqk', qblk, k,
                       preferred_element_type=jnp.float32) * scale
        dist = jnp.abs(qpos[:, None] - pos[None, :])
        s = s - slopes[:, None, None, None] * dist
        p = jax.nn.softmax(s, axis=-1)
        a = (p[:, :, 0] - lam * p[:, :, 1]).astype(v.dtype)
        return jnp.einsum('bhqk,bkhe->bqhe', a, v)

    o = lax.map(block, (qb, pb))
    o = o.swapaxes(0, 1).reshape(B, S, ATT_HEADS, ATT_V_DIM)
    o = rms_norm(o, gain) * (1.0 - lambda_init)
    return o.reshape(B, S, ATT_WIDTH)


def gla_direction(q, k, v, logf):
    B, S, H, DK = q.shape
    DV = v.shape[-1]
    L = REC_CHUNK
    nc = S // L

    def chunks(t):
        return t.astype(jnp.float32).reshape(B, nc, L, H, t.shape[-1]).transpose(1, 0, 3, 2, 4)

    tril = jnp.tril(jnp.ones((L, L), dtype=bool))[:, :, None]

    def step(state, inp):
        qc, kc, vc, gc = inp
        b = jnp.cumsum(gc, axis=2)
        b_last = b[:, :, -1:, :]
        o_inter = jnp.einsum('bhtk,bhkv->bhtv', qc * jnp.exp(b), state)
        rel = b[:, :, :, None, :] - b[:, :, None, :, :]
        decay = jnp.exp(jnp.where(tril, rel, -jnp.inf))
        scores = jnp.einsum('bhtk,bhsk,bhtsk->bhts', qc, kc, decay)
        o_intra = jnp.einsum('bhts,bhsv->bhtv', scores, vc)
        state = (jnp.exp(b_last).swapaxes(-1, -2) * state
                 + jnp.einsum('bhsk,bhsv->bhkv', kc * jnp.exp(b_last - b), vc))
        return state, o_inter + o_intra

    s0 = jnp.zeros((B, H, DK, DV), jnp.float32)
    _, o = lax.scan(step, s0, (chunks(q), chunks(k), chunks(v), chunks(logf)))
    return o.transpose(1, 0, 3, 2, 4).reshape(B, S, H, DV)


def hgrn2_mixer(q, f_fwd, f_bwd, i, g, lb_fwd, lb_bwd, gain):
    B, S, _ = q.shape
    heads = lambda t, d: t.reshape(B, S, REC_HEADS, d)
    qh = heads(q, REC_K_DIM).astype(jnp.float32) * (REC_K_DIM ** -0.5)
    ih = heads(i, REC_V_DIM)

    def gates(z, lb):
        lb = lb.astype(jnp.float32).reshape(REC_HEADS, REC_K_DIM)
        logf = jnp.logaddexp(jnp.log(lb), jnp.log1p(-lb)
                             + jax.nn.log_sigmoid(heads(z, REC_K_DIM).astype(jnp.float32)))
        return -jnp.expm1(logf), logf

    kf, lf = gates(f_fwd, lb_fwd)
    kb, lbk = gates(f_bwd, lb_bwd)
    o_f = gla_direction(qh, kf, ih, lf)
    flip = lambda t: jnp.flip(t, axis=1)
    o_b = flip(gla_direction(flip(qh), flip(kb), flip(ih), flip(lbk)))
    o = rms_norm((o_f + o_b).astype(q.dtype), gain).reshape(B, S, REC_WIDTH)
    return o * jax.nn.silu(g)


def setup_inputs(seed: int = 0) -> dict:
    key = jax.random.key(seed)
    ks = jax.random.split(key, 21)
    f32 = jnp.float32
    nrm = lambda k, shape: jax.random.normal(k, shape, f32)
    return {
        "x": nrm(ks[0], (BATCH, SEQ, D_MODEL)),
        "norm1_g": 1.0 + 0.02 * nrm(ks[1], (DEPTH, D_MODEL)),
        "w_in": nrm(ks[2], (DEPTH, D_MODEL, IN_COLS)) * D_MODEL ** -0.5,
        "pool_w": nrm(ks[3], (DEPTH, N_POOL_GROUPS, POOL_GROUP_DIM, POOL_GROUP_DIM)) * POOL_GROUP_DIM ** -0.5,
        "pool_scale": 1.0 + 0.1 * nrm(ks[4], (DEPTH, POOL_WIDTH)),
        "lam_q1": 0.1 * nrm(ks[5], (DEPTH, ATT_QK_DIM)),
        "lam_k1": 0.1 * nrm(ks[6], (DEPTH, ATT_QK_DIM)),
        "lam_q2": 0.1 * nrm(ks[7], (DEPTH, ATT_QK_DIM)),
        "lam_k2": 0.1 * nrm(ks[8], (DEPTH, ATT_QK_DIM)),
        "diff_norm_g": 1.0 + 0.02 * nrm(ks[9], (DEPTH, ATT_V_DIM)),
        "hgrn_lb": 0.5 * nrm(ks[10], (2, DEPTH, REC_K_WIDTH)),
        "hgrn_norm_g": 1.0 + 0.02 * nrm(ks[11], (DEPTH, REC_V_DIM)),
        "w_up_pool": nrm(ks[12], (DEPTH, POOL_WIDTH, D_MODEL)) * POOL_WIDTH ** -0.5,
        "w_up_attn": nrm(ks[13], (DEPTH, ATT_WIDTH, D_MODEL)) * ATT_WIDTH ** -0.5,
        "w_up_rec": nrm(ks[14], (DEPTH, REC_WIDTH, D_MODEL)) * REC_WIDTH ** -0.5,
        "w_out": nrm(ks[15], (DEPTH, D_MODEL, D_MODEL)) * D_MODEL ** -0.5,
        "norm2_g": 1.0 + 0.02 * nrm(ks[16], (DEPTH, D_MODEL)),
        "w_ffn_in": nrm(ks[17], (DEPTH, D_MODEL, 2 * FFN_HIDDEN)) * D_MODEL ** -0.5,
        "w_ffn_out": nrm(ks[18], (DEPTH, FFN_HIDDEN, D_MODEL)) * FFN_HIDDEN ** -0.5,
        "final_norm_g": 1.0 + 0.02 * nrm(ks[19], (D_MODEL,)),
    }


def reference(x, norm1_g, w_in, pool_w, pool_scale, lam_q1, lam_k1, lam_q2, lam_k2,
              diff_norm_g, hgrn_lb, hgrn_norm_g, w_up_pool, w_up_attn, w_up_rec, w_out,
              norm2_g, w_ffn_in, w_ffn_out, final_norm_g):
    B, S, _ = x.shape
    lb_all = jnp.cumsum(jax.nn.softmax(hgrn_lb.astype(jnp.float32), axis=1), axis=1)
    lb_all = lb_all - lb_all[:, :1]
    for l in range(DEPTH):
        h = rms_norm(x, norm1_g[l])
        proj = h @ w_in[l]
        u_pool, aq, ak, av, rq, rf, rb, ri, rg, gate_pre = split_cols(proj)

        y_pool = pool_mixer(u_pool, pool_w[l], pool_scale[l])

        lambda_init = 0.8 - 0.6 * math.exp(-0.3 * l)
        lam = (jnp.exp(jnp.sum(lam_q1[l].astype(jnp.float32) * lam_k1[l].astype(jnp.float32)))
               - jnp.exp(jnp.sum(lam_q2[l].astype(jnp.float32) * lam_k2[l].astype(jnp.float32)))
               + lambda_init)
        y_attn = diff_attention(aq, ak, av, lam, diff_norm_g[l], lambda_init)

        y_rec = hgrn2_mixer(rq, rf, rb, ri, rg, lb_all[0, l], lb_all[1, l], hgrn_norm_g[l])

        gates = jax.nn.sigmoid(gate_pre.reshape(B, S, N_BRANCHES, D_MODEL))
        merged = (gates[:, :, 0] * (y_pool @ w_up_pool[l])
                  + gates[:, :, 1] * (y_attn @ w_up_attn[l])
                  + gates[:, :, 2] * (y_rec @ w_up_rec[l]))
        x = x + merged @ w_out[l]

        h2 = rms_norm(x, norm2_g[l])
        gu = h2 @ w_ffn_in[l]
        x = x + (jax.nn.silu(gu[..., :FFN_HIDDEN]) * gu[..., FFN_HIDDEN:]) @ w_ffn_out[l]
    return rms_norm(x, final_norm_g)
```

```python
import math
from contextlib import ExitStack
import numpy as np
import concourse.bass as bass
import concourse.mybir as mybir
from concourse.bass_utils import run_bass_kernel_spmd

F32 = mybir.dt.float32
BF = mybir.dt.bfloat16
F16 = mybir.dt.float16
AF = mybir.ActivationFunctionType
ALU = mybir.AluOpType
AX = mybir.AxisListType

D = 1024
NJ = 8
DEPTH = 4
FFN = 2816
NHC = 22
IN_COLS = 7680
EPS = 1e-6
C_POOL, C_AQ, C_AK, C_AV, C_RQ, C_RF, C_RB, C_RI, C_RG, C_GATE = 0, 512, 1024, 1536, 2048, 2560, 3072, 3584, 4096, 4608
SLOPES = [2.0 ** (-8.0 * (h + 1) / 4) for h in range(4)]
LCH = 32


class Buf:
    __slots__ = ("name", "lw", "rd")

    def __init__(self, name=""):
        self.name = name
        self.lw = []
        self.rd = {}


class Tl:
    __slots__ = ("ap", "bufs")

    def __init__(self, ap, bufs):
        self.ap = ap
        self.bufs = tuple(bufs)

    def __getitem__(self, idx):
        return Tl(self.ap[idx], self.bufs)


class Op:
    __slots__ = ("eng", "fn", "deps", "dma", "need_inc", "ticket", "sem", "semval", "idx")


ENGS = ("pe", "dve", "act", "pool", "sp")


class Prog:
    def __init__(self, nc, ndma_sems=24):
        self.nc = nc
        self.ops = {e: [] for e in ENGS}
        self.n = 0
        self.ndma = ndma_sems
        self.dma_hist = {e: [] for e in ENGS}

    def op(self, eng, fn, reads=(), writes=(), dma=False):
        o = Op()
        o.eng, o.fn, o.dma, o.need_inc, o.ticket, o.sem, o.semval = eng, fn, dma, False, 0, None, 0
        o.idx = self.n
        self.n += 1
        deps = {}
        rb, wb = [], []
        for t in reads:
            if t is not None:
                rb.extend(t.bufs)
        for t in writes:
            wb.extend(t.bufs)
        for b in rb:
            for w_ in b.lw:
                deps[id(w_)] = w_
        for b in wb:
            for w_ in b.lw:
                deps[id(w_)] = w_
            for r in b.rd.values():
                deps[id(r)] = r
        key = ("dma", o.idx) if dma else eng
        for b in rb:
            b.rd[key] = o
        for b in wb:
            b.lw = [o]
            b.rd = {}
        if dma:
            h = self.dma_hist[eng]
            if len(h) >= self.ndma:
                p = h[len(h) - self.ndma]
                deps[id(p)] = p
            h.append(o)
        dl = []
        for d in deps.values():
            if d is o:
                continue
            if d.eng == "pe" and eng == "pe" and not d.dma and not dma:
                continue
            dl.append(d)
        o.deps = dl
        self.ops[eng].append(o)
        return o

    def emit(self, block_engines, sems, dma_sems):
        nc = self.nc
        for e in ENGS:
            for o in self.ops[e]:
                for d in o.deps:
                    d.need_inc = True
        for e in ENGS:
            cnt = 0
            k = 0
            for o in self.ops[e]:
                if o.dma:
                    pool = dma_sems[e]
                    o.sem = pool[k % len(pool)]
                    o.semval = 16 * (k // len(pool) + 1)
                    k += 1
                elif o.need_inc:
                    cnt += 1
                    o.ticket = cnt
                    o.sem = sems[e]
                    o.semval = cnt
        for e in ENGS:
            eng = block_engines[e]
            seen = {}
            for o in self.ops[e]:
                waits = {}
                for d in o.deps:
                    sid = id(d.sem)
                    if sid not in waits or waits[sid][1] < d.semval:
                        waits[sid] = (d.sem, d.semval)
                for sid, (s, v) in waits.items():
                    if seen.get(sid, 0) >= v:
                        continue
                    seen[sid] = v
                    eng.wait_ge(s, v)
                if o.fn is None:
                    continue
                ins = o.fn(eng)
                if o.dma:
                    ins.then_inc(o.sem, 16)
                elif o.need_inc:
                    ins.then_inc(o.sem, 1)


class K:
    def __init__(self, P):
        self.P = P

    @staticmethod
    def _a(x):
        return x.ap if isinstance(x, Tl) else x

    def mm(self, out, lhsT, rhs, start=True, stop=True, **kw):
        self.P.op("pe", lambda e: e.matmul(out.ap, lhsT.ap, rhs.ap, start=start, stop=stop, **kw),
                  reads=[lhsT, rhs], writes=[out])

    def tr(self, out, in_, ident):
        self.P.op("pe", lambda e: e.transpose(out.ap, in_.ap, ident.ap), reads=[in_, ident], writes=[out])

    def act(self, out, in_, func, bias=None, scale=None, accum=None):
        kw = {}
        rd = [in_]
        if bias is not None:
            kw["bias"] = self._a(bias)
            if isinstance(bias, Tl):
                rd.append(bias)
        if scale is not None:
            kw["scale"] = self._a(scale)
            if isinstance(scale, Tl):
                rd.append(scale)
        wr = [out]
        if accum is not None:
            kw["accum_out"] = accum.ap
            wr.append(accum)
        self.P.op("act", lambda e: e.activation(out.ap, in_.ap, func, **kw), reads=rd, writes=wr)

    def ts(self, eng, out, in0, s1, s2, op0, op1=None):
        rd = [in0] + [s for s in (s1, s2) if isinstance(s, Tl)]
        a1, a2 = self._a(s1), self._a(s2)
        if op1 is None:
            f = lambda e: e.tensor_scalar(out.ap, in0.ap, a1, a2, op0)
        else:
            f = lambda e: e.tensor_scalar(out.ap, in0.ap, a1, a2, op0, op1)
        self.P.op(eng, f, reads=rd, writes=[out])

    def stt(self, out, in0, scalar, in1, op0, op1):
        rd = [in0, in1] + ([scalar] if isinstance(scalar, Tl) else [])
        a = self._a(scalar)
        self.P.op("dve", lambda e: e.scalar_tensor_tensor(out.ap, in0.ap, a, in1.ap, op0, op1), reads=rd, writes=[out])

    def tt(self, eng, out, in0, in1, op):
        self.P.op(eng, lambda e: e.tensor_tensor(out.ap, in0.ap, in1.ap, op), reads=[in0, in1], writes=[out])

    def cp(self, eng, out, in_):
        if eng == "act":
            self.P.op("act", lambda e: e.copy(out.ap, in_.ap), reads=[in_], writes=[out])
        else:
            self.P.op(eng, lambda e: e.tensor_copy(out.ap, in_.ap), reads=[in_], writes=[out])

    def recip(self, out, in_):
        self.P.op("dve", lambda e: e.reciprocal(out.ap, in_.ap), reads=[in_], writes=[out])

    def memset(self, eng, out, val):
        self.P.op(eng, lambda e: e.memset(out.ap, val), reads=[], writes=[out])

    def reduce(self, out, in_, op, axis=AX.X):
        self.P.op("dve", lambda e: e.tensor_reduce(out.ap, in_.ap, axis, op), reads=[in_], writes=[out])

    def scan(self, out, d0, d1, init, op0, op1):
        self.P.op("dve", lambda e: e.tensor_tensor_scan(out.ap, d0.ap, d1.ap, init, op0, op1), reads=[d0, d1], writes=[out])

    def dma(self, eng, out, in_):
        self.P.op(eng, lambda e: e.dma_start(out=out.ap, in_=in_.ap), reads=[in_], writes=[out], dma=True)


def mkap(t, offset, pairs):
    a = t.ap if isinstance(t, Tl) else t
    return bass.AP(tensor=a.tensor, offset=offset, ap=[list(p) for p in pairs])


def build(S=2048, NL=DEPTH, NSEQ=2, layers=None, debug=False, final=None):
    NT = S // 512
    NKC = S // 128
    NHALF = 2 if S >= 1024 else 1
    HT = S // NHALF
    HNT = HT // 512
    if layers is None:
        layers = list(range(NL))
    nc = bass.Bass("TRN2", target_bir_lowering=False)
    P = Prog(nc)
    k = K(P)
    dram = {}

    def din(name, shape, dt=F32):
        dram[name] = nc.dram_tensor(name, list(shape), dt, kind="ExternalInput").ap()
        return dram[name]

    x_d = din("x", [NSEQ, S, D])
    w_in_d = din("w_in", [DEPTH, D, IN_COLS])
    pool_w_d = din("pool_w", [DEPTH, 4, 128, 128])
    w_up_d = [din("w_up_pool", [DEPTH, 512, D]), din("w_up_attn", [DEPTH, 512, D]), din("w_up_rec", [DEPTH, 512, D])]
    w_out_d = din("w_out", [DEPTH, D, D])
    w_fi_d = din("w_ffn_in", [DEPTH, D, 2 * FFN])
    w_fo_d = din("w_ffn_out", [DEPTH, FFN, D])
    g1_d = din("g1T", [128, DEPTH * 8])
    g2_d = din("g2T", [128, DEPTH * 8])
    gf_d = din("gfT", [128, 8])
    psc_d = din("pscT", [128, DEPTH * 4])
    dng_d = din("dng", [128, DEPTH * 128])
    hng_d = din("hngT", [128, DEPTH])
    lamv_d = din("lamv", [128, 4 * DEPTH * 64])
    lb_d = din("lbT", [128, 2 * 4 * DEPTH])
    identf_d = din("identf", [128, 128])
    tri_d = din("tri", [128, 256])
    smask_d = din("smask", [128, 512])
    TBW = S + (NKC - 1) * 128
    alibi_d = din("alibi", [128, TBW])
    y_d = nc.dram_tensor("y", [NSEQ, S, D], F32, kind="ExternalOutput").ap()
    if debug:
        dbg_d = nc.dram_tensor("dbg", [128, 12 * S], BF, kind="ExternalOutput").ap()

    es = ExitStack()
    with es:
        def sb(name, shape, dt):
            return es.enter_context(nc.sbuf_tensor(name, list(shape), dt))

        X = sb("X", [128, NJ, S], F32)
        H = sb("H", [128, NJ, S], BF)
        Y = sb("Y", [128, 12 * S], BF)
        NWS = 4
        WS = [sb(f"WS{i}", [128, 2048], BF) for i in range(NWS)]
        CF = sb("CF", [128, 1024], F32)
        IDF = sb("IDF", [128, 128], F32)
        IDB = sb("IDB", [128, 128], BF)
        ONB = sb("ONB", [128, 128], BF)
        ZRB = sb("ZRB", [128, 512], BF)
        TRI = sb("TRI", [128, 256], BF)
        SMK = sb("SMK", [128, 512], F32)
        TMP = sb("TMP", [128, 38 * 1024 // 4], F32)
        PS = [es.enter_context(nc.psum_tensor(f"ps{i}", [128, 512], F32)) for i in range(8)]
        sems = {e: es.enter_context(nc.semaphore(f"s_{e}")) for e in ENGS}
        dma_sems = {e: [es.enter_context(nc.semaphore(f"d_{e}{i}")) for i in range(P.ndma)] for e in ("sp", "pool")}
        dma_sems["act"] = dma_sems["sp"]
        dma_sems["pe"] = dma_sems["sp"]
        dma_sems["dve"] = dma_sems["sp"]

        bX = [[Buf() for _ in range(NT)] for _ in range(NJ)]
        bH = [[Buf() for _ in range(NT)] for _ in range(NJ)]
        bY = [[Buf() for _ in range(NT)] for _ in range(12)]
        bPS = [Buf(f"ps{i}") for i in range(8)]
        bWS = [Buf() for _ in range(NWS)]
        bC = Buf("consts")

        def Xt(j, tt):
            return Tl(X[:, j, tt * 512:(tt + 1) * 512], [bX[j][tt]])

        def Ht(j, tt):
            return Tl(H[:, j, tt * 512:(tt + 1) * 512], [bH[j][tt]])

        Yv = Y[:, :].rearrange("p (c t) -> p c t", t=S)

        def Yt(c, tt):
            return Tl(Yv[:, c, tt * 512:(tt + 1) * 512], [bY[c][tt]])

        def ps(i):
            return Tl(PS[i][:, :], [bPS[i]])

        def psb(i):
            return Tl(PS[i][:, :].bitcast(BF), [bPS[i]])

        class Arena:
            def __init__(self):
                self.off = 0

            def reset(self):
                self.off = 0

            def get(self, nbytes, dt, shape=None):
                assert self.off % 4 == 0
                n32 = (nbytes + 3) // 4
                assert self.off // 4 + n32 <= 38 * 1024 // 4, "TMP arena overflow"
                a = TMP[:, self.off // 4: self.off // 4 + n32]
                self.off += n32 * 4
                if dt != F32:
                    a = a.bitcast(dt)
                return a

        AR = Arena()
        arena_cur = []
        arena_old = []

        _areset = AR.reset

        def areset():
            arena_old.extend(arena_cur)
            del arena_cur[:]
            if len(arena_old) > 400:
                del arena_old[:len(arena_old) - 400]
            _areset()

        AR.reset = areset

        norm_recs = []

        def tmp_fixed(off, nbytes, dt):
            n32 = (nbytes + 3) // 4
            a = TMP[:, off // 4: off // 4 + n32]
            if dt != F32:
                a = a.bitcast(dt)
            nb = Buf(f"fix{off}")
            norm_recs.append((off, off + n32 * 4, nb))
            return Tl(a, [nb])

        def norm_refresh():
            for (f0, f1, fb) in norm_recs:
                for (a0, a1, ob) in arena_old + arena_cur:
                    if a0 < f1 and f0 < a1:
                        for w_ in ob.lw:
                            if all(w_ is not q for q in fb.lw):
                                fb.lw.append(w_)
                        for kk, r in ob.rd.items():
                            if kk not in fb.rd or fb.rd[kk].idx < r.idx:
                                fb.rd[kk] = r

        def tmp(nbytes, dt):
            off = AR.off
            ap = AR.get(nbytes, dt)
            end = AR.off
            nb = Buf(f"tmp{off}")
            for (a0, a1, ob) in arena_old + norm_recs:
                if a0 < end and off < a1:
                    for w_ in ob.lw:
                        if all(w_ is not q for q in nb.lw):
                            nb.lw.append(w_)
                    for kk, r in ob.rd.items():
                        if kk not in nb.rd or nb.rd[kk].idx < r.idx:
                            nb.rd[kk] = r
            arena_cur.append((off, end, nb))
            return Tl(ap, [nb])

        cfo = [0]

        def cf(n):
            o = cfo[0]
            cfo[0] += n
            assert cfo[0] <= 1024
            return Tl(CF[:, o:o + n], [bC])

        G1, G2, GF = cf(32), cf(32), cf(8)
        PSC, HNG = cf(16), cf(4)
        LAMT, NEGLAM = cf(4), cf(4)
        LBT, LBV, OML = cf(32), cf(32), cf(32)
        EPST = cf(1)
        EPSA = cf(4)
        RCE = cf(64)
        SMALL = Tl(cf(64).ap, [Buf('small')])
        IDFt = Tl(IDF[:, :], [Buf()])
        IDBt = Tl(IDB[:, :], [Buf()])
        ONBt = Tl(ONB[:, :], [Buf()])
        ZRBt = Tl(ZRB[:, :], [Buf()])
        TRIt = Tl(TRI[:, :], [Buf()])
        SMKt = Tl(SMK[:, :], [Buf()])
        ext = lambda ap: Tl(ap, [])

        k.dma("sp", G1, ext(g1_d))
        k.dma("sp", G2, ext(g2_d))
        k.dma("sp", GF, ext(gf_d))
        k.dma("sp", PSC, ext(psc_d))
        k.dma("sp", HNG, ext(hng_d))
        k.dma("sp", LBT, ext(lb_d))
        k.dma("sp", IDFt, ext(identf_d))
        k.dma("sp", SMKt, ext(smask_d))
        k.dma("pool", TRIt, ext(tri_d))
        k.cp("dve", IDBt, IDFt)
        k.memset("dve", ONBt, 1.0)
        k.memset("dve", ZRBt, 0.0)
        k.memset("dve", EPST, EPS)
        lam_init = [0.8 - 0.6 * math.exp(-0.3 * l) for l in range(DEPTH)]
        for l in range(DEPTH):
            k.memset("dve", EPSA[:, l:l + 1], EPS / (1.0 - lam_init[l]) ** 2)
        for g in range(4):
            hw = 2 ** g
            for i in range(hw):
                k.memset("dve", RCE[:, g * 16 + i:g * 16 + i + 1], 1.0 / (i + hw))
            for i in range(hw - 1):
                k.memset("dve", RCE[:, g * 16 + 8 + i:g * 16 + 9 + i], 1.0 / (2 * hw - 1 - i))
        AR.reset()
        LV = tmp(4 * DEPTH * 64 * 4, F32)
        k.dma("sp", LV, ext(lamv_d))
        PR = tmp(2 * DEPTH * 64 * 4, F32)
        n1 = DEPTH * 64
        k.tt("dve", PR[:, 0:n1], LV[:, 0:n1], LV[:, n1:2 * n1], ALU.mult)
        k.tt("dve", PR[:, n1:2 * n1], LV[:, 2 * n1:3 * n1], LV[:, 3 * n1:4 * n1], ALU.mult)
        S12 = SMALL[:, 0:8]
        k.reduce(S12, Tl(PR.ap.rearrange("p (a d) -> p a d", d=64), PR.bufs), ALU.add)
        E12 = SMALL[:, 8:16]
        k.act(E12, S12, AF.Exp)
        k.tt("dve", LAMT, E12[:, 0:4], E12[:, 4:8], ALU.subtract)
        for l in range(DEPTH):
            k.ts("dve", LAMT[:, l:l + 1], LAMT[:, l:l + 1], lam_init[l], None, ALU.add)
        k.ts("dve", NEGLAM, LAMT, -1.0, None, ALU.mult)
        ELB = SMALL[:, 16:48]
        k.act(ELB, LBT, AF.Exp)
        SLB = SMALL[:, 48:56]
        k.reduce(SLB, Tl(ELB.ap.rearrange("p (a l) -> p a l", l=DEPTH), ELB.bufs), ALU.add)
        k.recip(SLB, SLB)
        SMX = LBT
        k.tt("dve", Tl(SMX.ap.rearrange("p (a l) -> p a l", l=DEPTH), SMX.bufs),
             Tl(ELB.ap.rearrange("p (a l) -> p a l", l=DEPTH), ELB.bufs),
             Tl(mkap(SLB, SLB.ap.offset, [SLB.ap.ap[0], [1, 8], [0, DEPTH]]), SLB.bufs), ALU.mult)
        smv = SMX.ap.rearrange("p (a l) -> p a l", l=DEPTH)
        lbv = LBV.ap.rearrange("p (a l) -> p a l", l=DEPTH)
        k.memset("dve", Tl(lbv[:, :, 0:1], LBV.bufs), 0.0)
        k.cp("dve", Tl(lbv[:, :, 1:2], LBV.bufs), Tl(smv[:, :, 1:2], SMX.bufs))
        for l in range(2, DEPTH):
            k.tt("dve", Tl(lbv[:, :, l:l + 1], LBV.bufs), Tl(lbv[:, :, l - 1:l], LBV.bufs), Tl(smv[:, :, l:l + 1], SMX.bufs), ALU.add)
        k.ts("dve", OML, LBV, -1.0, 1.0, ALU.mult, ALU.add)

        ws_i = [0]

        def wslot():
            i = ws_i[0] % NWS
            ws_i[0] += 1
            return i

        def wview(i, nk, ncols, off=0):
            assert off + nk * ncols <= 2048
            return Tl(WS[i][:, off:off + nk * ncols].rearrange("p (k c) -> p k c", c=ncols), [bWS[i]])

        def wload(i, src2d, nk, c0, ncols, dst_c0=0, tot_cols=None, off=0, k0=0):
            tot = tot_cols or ncols
            v = wview(i, nk, tot, off)
            srcv = src2d.rearrange("(k p) c -> p k c", p=128)[:, k0:k0 + nk, c0:c0 + ncols]
            k.dma("pool", v[:, :, dst_c0:dst_c0 + ncols], ext(srcv))
            return v

        def proj_fm(out_ps, w, wc0, tt, nk=8, rhs_fn=None):
            for kc in range(nk):
                rhs = Ht(kc, tt) if rhs_fn is None else rhs_fn(kc)
                k.mm(out_ps, w[:, kc, wc0:wc0 + 128], rhs, start=(kc == 0), stop=(kc == nk - 1))

        NOFF = 32 * 1024
        NSQ = [tmp_fixed(NOFF, 1024, BF), tmp_fixed(NOFF + 1024, 1024, BF)]
        NRS = [tmp_fixed(NOFF + 2048, 2048, F32), tmp_fixed(NOFF + 4096, 2048, F32)]
        npend = []

        def ndrain(n_):
            for _ in range(min(n_, len(npend))):
                npend.pop(0)()

        def rmsnorm_ops(G, l, tts):
            norm_refresh()
            ops = []
            for tt in tts:
                pb = 6 + (tt % 2)
                rs = NRS[tt % 2]
                for j in range(NJ):
                    sq = NSQ[j % 2]

                    def f1(j=j, tt=tt, sq=sq, pb=pb):
                        k.act(sq, Xt(j, tt), AF.Square)
                        k.mm(ps(pb), ONBt, sq, start=(j == 0), stop=(j == NJ - 1))
                    ops.append(f1)

                def f2(rs=rs, pb=pb):
                    k.act(rs, ps(pb), AF.Sqrt, bias=EPST, scale=1.0 / D)
                    k.recip(rs, rs)
                ops.append(f2)
                for j in range(NJ):
                    ops.append(lambda j=j, tt=tt, rs=rs: k.stt(Ht(j, tt), Xt(j, tt), G[:, l * 8 + j:l * 8 + j + 1], rs,
                                                                 ALU.mult, ALU.mult))
            return ops

        out_bufs = []
        for s in range(NSEQ):
            AR.reset()
            XS = [tmp(4096, F32), tmp(4096, F32)]
            n = 0
            for tt in range(NT):
                for tb in range(4):
                    xs = XS[n % 2]
                    n += 1
                    t0 = tt * 512 + tb * 128
                    k.dma("sp", xs, ext(x_d[s, t0:t0 + 128, :]))
                    for j in range(NJ):
                        k.tr(ps(j)[:, tb * 128:(tb + 1) * 128], xs[:, j * 128:(j + 1) * 128], IDFt)
                for j in range(NJ):
                    k.cp("act" if j % 2 else "dve", Xt(j, tt), ps(j))

            for l in layers:
                wl = w_in_d[l]
                AR.reset()
                ndrain(len(npend))
                n1_tts = list(range(NT)) if (l == layers[0] or NHALF == 1) else list(range(HNT, NT))
                for f_ in rmsnorm_ops(G1, l, n1_tts):
                    f_()

                AR.reset()
                WD = S + 24
                U = tmp(WD * 4, F32)
                A_ = tmp(WD * 4, F32)
                B_ = tmp(WD * 4, F32)
                DT = tmp(S * 2, BF)
                PWt = tmp(4 * 128 * 2, BF)
                k.memset("pool", U[:, 0:8], 0.0)
                k.memset("pool", U[:, 8 + S:WD], 0.0)
                pwv = Tl(PWt.ap.rearrange("p (g d) -> p g d", d=128), PWt.bufs)
                k.dma("pool", pwv, ext(pool_w_d[l].rearrange("g c d -> c g d")))
                aw = {}

                def attn_weights(hd):
                    wi = wslot()
                    wv_ = wview(wi, 8, 256)
                    wload(wi, wl, 8, C_AQ + hd * 128, 128, 0, 256)
                    wload(wi, wl, 8, C_AK + hd * 128, 128, 128, 256)
                    wvv = wload(wslot(), wl, 8, C_AV + hd * 128, 128)
                    aw[hd] = (wv_, wvv)

                wps = [wload(wslot(), wl, 8, C_POOL + g * 128, 128) for g in range(4)]
                for g in range(4):
                    w = 2 ** (g + 1)
                    hw = w // 2
                    wp = wps[g]
                    if g == 2:
                        attn_weights(0)
                    for tt in range(NT):
                        pb = tt % 4
                        proj_fm(ps(pb), wp, 0, tt)
                        k.cp("act", U[:, 8 + tt * 512:8 + (tt + 1) * 512], ps(pb))
                    src, dst = U, A_
                    step = 1
                    while step < w:
                        nn = WD - (2 * step - 1)
                        k.tt("pool", dst[:, 0:nn], src[:, 0:nn], src[:, step:step + nn], ALU.add)
                        src, dst = dst, (B_ if dst is A_ else A_)
                        step *= 2
                    sw = src
                    o0 = 8 - hw
                    k.stt(DT[:, 0:S], sw[:, o0:o0 + S], 1.0 / w, U[:, 8:8 + S], ALU.mult, ALU.subtract)
                    ET = SMALL[:, 0:8]
                    k.tt("dve", ET[:, 0:hw], sw[:, o0:o0 + hw], RCE[:, g * 16:g * 16 + hw], ALU.mult)
                    k.tt("dve", DT[:, 0:hw], ET[:, 0:hw], U[:, 8:8 + hw], ALU.subtract)
                    if hw > 1:
                        t0 = S - hw + 1
                        ne = hw - 1
                        k.tt("dve", ET[:, 0:ne], sw[:, o0 + t0:o0 + t0 + ne], RCE[:, g * 16 + 8:g * 16 + 8 + ne], ALU.mult)
                        k.tt("dve", DT[:, t0:t0 + ne], ET[:, 0:ne], U[:, 8 + t0:8 + t0 + ne], ALU.subtract)
                    for tt in range(NT):
                        pb = 4 + tt % 2
                        k.mm(ps(pb), pwv[:, g, :], DT[:, tt * 512:(tt + 1) * 512])
                        k.act(Yt(g, tt), ps(pb), AF.Identity, scale=PSC[:, l * 4 + g:l * 4 + g + 1])

                AR.reset()
                KT = tmp(S * 2, BF)
                VA = tmp(NKC * 132 * 2, BF)
                VAv = Tl(VA.ap.rearrange("p (c e) -> p c e", e=132), VA.bufs)
                TBL = tmp(TBW * 2, F16)
                NR = 3
                TM = [tmp(2048, F32) for _ in range(NR)]
                PT = [tmp(1024, BF) for _ in range(NR)]
                QA = [[tmp(1024, BF), tmp(1024, BF)] for _ in range(2)]
                OS = tmp(1032 * 4, F32)
                OBq = [tmp(512, F32) for _ in range(4)]
                T1q = [tmp(512, F32) for _ in range(2)] * 2
                YTq = [tmp(256, BF) for _ in range(4)]
                SMq = [tmp(32, F32) for _ in range(4)]
                DNG = tmp(512, F32)
                NH5 = tmp(4, F32)
                NH1 = tmp(4, F32)
                k.dma("pool", TBL, ext(alibi_d))
                k.dma("sp", DNG, ext(dng_d[:, l * 128:(l + 1) * 128]))
                k.memset("dve", VAv[:, :, 128:129], 1.0)
                for p_ in range(2):
                    k.memset("pool", QA[p_][0][64:128, :], 0.0)
                    k.memset("pool", QA[p_][1][0:64, :], 0.0)
                k.memset("pool", NH5, -0.5)
                k.memset("pool", NH1, -1.0)
                base = (NKC - 1) * 128
                pend = []

                def drain(nops):
                    for _ in range(min(nops, len(pend))):
                        pend.pop(0)()

                def oacc(c, qb):
                    idx = c * 4 + qb
                    return ps(5 + idx // 3)[:, (idx % 3) * 129:(idx % 3) * 129 + 129]

                def osv(c, qb):
                    idx = c * 4 + qb
                    return OS[:, idx * 129:idx * 129 + 129]

                qcount = [0]

                def qproj_ops(wv_, qt, par):
                    ops = []
                    for kc in range(8):
                        ops.append(lambda kc=kc: k.mm(ps(3), wv_[:, kc, 0:128], Ht(kc, qt), start=(kc == 0), stop=(kc == 7)))
                    ops.append(lambda: k.cp("act", QA[par][0][0:64, :], ps(3)[0:64, :]))
                    ops.append(lambda: k.cp("act", QA[par][1][64:128, :], ps(3)[64:128, :]))
                    return ops

                def epilogue_ops(hd, qt):
                    ops = []
                    sc_ = 1.0 / (128.0 * (1.0 - lam_init[l]) ** 2)
                    stages = [
                        lambda qb: k.tt("pool", SMq[qb][:, 0:1], osv(0, qb)[:, 128:129], NH1, ALU.pow),
                        lambda qb: k.tt("pool", SMq[qb][:, 1:2], osv(1, qb)[:, 128:129], NH1, ALU.pow),
                        lambda qb: k.ts("pool", SMq[qb][:, 2:3], SMq[qb][:, 1:2], NEGLAM[:, l:l + 1], None, ALU.mult),
                        lambda qb: k.ts("pool", T1q[qb], osv(1, qb)[:, 0:128], SMq[qb][:, 2:3], None, ALU.mult),
                        lambda qb: k.stt(OBq[qb], osv(0, qb)[:, 0:128], SMq[qb][:, 0:1], T1q[qb], ALU.mult, ALU.add),
                        lambda qb: k.act(T1q[qb], OBq[qb], AF.Square, accum=SMq[qb][:, 3:4]),
                        lambda qb: k.ts("dve", SMq[qb][:, 4:5], SMq[qb][:, 3:4], sc_, EPSA[:, l:l + 1], ALU.mult, ALU.add),
                        lambda qb: k.tt("pool", SMq[qb][:, 5:6], SMq[qb][:, 4:5], NH5, ALU.pow),
                        lambda qb: k.stt(Tl(YTq[qb].ap, YTq[qb].bufs), OBq[qb], SMq[qb][:, 5:6], DNG, ALU.mult, ALU.mult),
                        lambda qb: k.tr(psb(4)[:, qb * 128:(qb + 1) * 128], YTq[qb], IDBt),
                    ]
                    order = []
                    for si in (0, 1, 2):
                        order += [(si, qb) for qb in range(4)]
                    order += [(3, 0), (3, 1), (4, 0), (4, 1), (3, 2), (3, 3), (4, 2), (4, 3)]
                    for si in range(5, len(stages)):
                        order += [(si, qb) for qb in range(4)]
                    for si, qb in order:
                        ops.append(lambda st_=stages[si], qb=qb: st_(qb))
                    ops.append(lambda: k.cp("act", Yt(4 + hd, qt), psb(4)[:, 0:512]))
                    return ops

                for hd in range(4):
                    wv_, wvv = aw[hd]
                    for tt in range(NT):
                        proj_fm(ps(tt % 3), wv_, 128, tt)
                        k.cp("dve", KT[:, tt * 512:(tt + 1) * 512], ps(tt % 3))
                        drain(6)
                    for tt in range(NT):
                        pb = tt % 3
                        for tb in range(4):
                            for kc in range(8):
                                k.mm(ps(pb)[:, tb * 128:(tb + 1) * 128], Ht(kc, tt)[:, tb * 128:(tb + 1) * 128],
                                     wvv[:, kc, 0:128], start=(kc == 0), stop=(kc == 7))
                        k.cp("act", VAv[:, tt * 4:(tt + 1) * 4, 0:128],
                             Tl(ps(pb).ap.rearrange("p (b e) -> p b e", e=128), ps(pb).bufs))
                        drain(6)
                    drain(len(pend))
                    nslope8 = -8.0 * SLOPES[hd]
                    par0 = qcount[0] % 2
                    for f_ in qproj_ops(wv_, 0, par0):
                        f_()
                    for qt in range(NT):
                        par = qcount[0] % 2
                        qcount[0] += 1
                        Qc = QA[par]
                        if qt + 1 < NT:
                            pend.extend(qproj_ops(wv_, qt + 1, (par + 1) % 2))
                        elif hd + 1 < 4:
                            attn_weights(hd + 1)
                        for bnk in range(5, 8):
                            k.mm(ps(bnk), ZRBt[:, 0:128], ZRBt, start=True, stop=True, skip_group_check=True)
                        its = [(kc, c) for kc in range(NKC) for c in range(2)]
                        nit = len(its)
                        SK = 2
                        for i in range(nit + SK):
                            if i < nit:
                                kc, c = its[i]
                                r = i % NR
                                sb_ = ps(i % 3)
                                k.mm(sb_, KT[:, kc * 128:(kc + 1) * 128], Qc[c])
                                off = qt * 512 - kc * 128 + base
                                k.stt(TM[r], TBL[:, off:off + 512], nslope8, sb_, ALU.mult, ALU.add)
                                k.act(PT[r], TM[r], AF.Exp, scale=0.125)
                            if i >= SK:
                                kc, c = its[i - SK]
                                r = (i - SK) % NR
                                for qb in range(4):
                                    k.mm(oacc(c, qb), PT[r][:, qb * 128:(qb + 1) * 128], VAv[:, kc, 0:129],
                                         start=False, stop=(kc == NKC - 1), skip_group_check=True)
                            rem = nit + SK - i
                            drain((len(pend) + rem - 1) // rem)
                        k.cp("act", OS[:, 0:387], ps(5)[:, 0:387])
                        k.cp("act", OS[:, 387:774], ps(6)[:, 0:387])
                        k.cp("act", OS[:, 774:1032], ps(7)[:, 0:258])
                        pend.extend(epilogue_ops(hd, qt))
                drain(len(pend))

                AR.reset()
                sgm, lgf, bb, xtra = [tmp(2048, F32) for _ in range(4)]
                key, Einv = sgm, bb
                Bbf = tmp(1024, BF)
                KdT = tmp(1024, BF)
                v4 = lambda t: Tl(t.ap.rearrange("p (b e) -> p b e", e=128), t.bufs)
                DB = []
                for _ in range(2):
                    DB.append(dict(A=tmp(1024, BF), V=v4(tmp(1024, BF)), ST=v4(tmp(1024, BF)), Kd=v4(tmp(1024, BF)),
                                   E1=tmp(2048, F32)))
                SBFr = tmp(17 * 128 * 2, BF)
                bS = [Buf() for _ in range(17)]
                SBs = lambda n_: Tl(SBFr.ap[:, n_ * 128:(n_ + 1) * 128], [bS[n_]])
                NS32 = 4
                S32 = [tmp(512, F32) for _ in range(NS32)]
                OACC = tmp(S * 4, F32)
                kap = 128 ** -0.5
                items = []
                for hd in range(4):
                    for dr in range(2):
                        segs = list(range(NT)) if dr == 0 else list(range(NT - 1, -1, -1))
                        for n_, sg in enumerate(segs):
                            items.append((hd, dr, sg, n_ == 0, (dr == 1 and n_ == NT - 1)))
                hw = {}
                st = {"sidx": 0}

                def hgrn_weights(hd):
                    wa = wslot()
                    wqf = wview(wa, 8, 256)
                    wload(wa, wl, 8, C_RQ + hd * 128, 128, 0, 256)
                    wload(wa, wl, 8, C_RF + hd * 128, 128, 128, 256)
                    wb = wslot()
                    wbi = wview(wb, 8, 256)
                    wload(wb, wl, 8, C_RB + hd * 128, 128, 0, 256)
                    wload(wb, wl, 8, C_RI + hd * 128, 128, 128, 256)
                    wgg = wload(wslot(), wl, 8, C_RG + hd * 128, 128)
                    hw[hd] = (wqf, wbi, wgg)

                def front1(i):
                    hd, dr, sg, first, last = items[i]
                    if hd not in hw:
                        hgrn_weights(hd)
                    wqf, wbi, wgg = hw[hd]
                    d = DB[i % 2]
                    proj_fm(ps(0), wqf, 0, sg)
                    if dr == 0:
                        proj_fm(ps(1), wqf, 128, sg)
                    else:
                        proj_fm(ps(1), wbi, 0, sg)
                    for tb in range(4):
                        for kc in range(8):
                            k.mm(ps(2)[:, tb * 128:(tb + 1) * 128], Ht(kc, sg)[:, tb * 128:(tb + 1) * 128],
                                 wbi[:, kc, 128:256], start=(kc == 0), stop=(kc == 7))

                def front2(i):
                    hd, dr, sg, first, last = items[i]
                    d = DB[i % 2]
                    lbc = LBV[:, (dr * 4 + hd) * DEPTH + l:(dr * 4 + hd) * DEPTH + l + 1]
                    omc = OML[:, (dr * 4 + hd) * DEPTH + l:(dr * 4 + hd) * DEPTH + l + 1]
                    E1 = d["E1"]
                    lastoff = LCH - 1 if dr == 0 else 0
                    rv = lambda t: Tl(mkap(t, t.ap.offset + 511, [t.ap.ap[0], [-1, 512]]), t.bufs)
                    elast = Tl(mkap(E1, E1.ap.offset + lastoff, [E1.ap.ap[0], [LCH, 16], [0, LCH]]), E1.bufs)
                    ops = [
                        lambda: k.act(sgm, ps(1), AF.Sigmoid),
                        lambda: k.cp("dve", d["V"], Tl(ps(2).ap.rearrange("p (b e) -> p b e", e=128), ps(2).bufs)),
                        lambda: k.cp("act", d["A"], ps(0)),
                        lambda: k.ts("dve", sgm, sgm, omc, lbc, ALU.mult, ALU.add),
                        lambda: k.act(lgf, sgm, AF.Ln),
                        lambda: k.ts("pool", key, sgm, -1.0, 1.0, ALU.mult, ALU.add),
                        (lambda: k.scan(bb, SMKt, lgf, 0.0, ALU.mult, ALU.add)) if dr == 0 else
                        (lambda: k.scan(rv(bb), SMKt, rv(lgf), 0.0, ALU.mult, ALU.add)),
                        lambda: k.act(E1, bb, AF.Exp),
                        lambda: k.act(Einv, bb, AF.Exp, scale=-1.0),
                        lambda: k.stt(d["A"], d["A"], kap, E1, ALU.mult, ALU.mult),
                        lambda: k.tt("pool", Bbf, key, Einv, ALU.mult),
                        lambda: k.tt("pool", Tl(KdT.ap.rearrange("p (c t) -> p c t", t=LCH), KdT.bufs),
                                     Tl(Bbf.ap.rearrange("p (c t) -> p c t", t=LCH), Bbf.bufs), elast, ALU.mult),
                    ]
                    return ops

                def front3(i):
                    hd, dr, sg, first, last = items[i]
                    d = DB[i % 2]
                    for tb in range(4):
                        k.tr(psb(3)[:, tb * 128:(tb + 1) * 128], KdT[:, tb * 128:(tb + 1) * 128], IDBt)
                    k.cp("act", d["Kd"], Tl(psb(3).ap[:, 0:512].rearrange("p (b e) -> p b e", e=128), psb(3).bufs))
                    for tb in range(4):
                        k.mm(ps(4)[:, tb * 128:(tb + 1) * 128], Bbf[:, tb * 128:(tb + 1) * 128],
                             d["A"][:, tb * 128:(tb + 1) * 128])
                    trm = TRIt[:, dr * 128:(dr + 1) * 128]
                    k.tt("dve", d["ST"], Tl(ps(4).ap.rearrange("p (b t) -> p b t", t=128), ps(4).bufs),
                         Tl(mkap(trm, trm.ap.offset, [trm.ap.ap[0], [0, 4], [1, 128]]), trm.bufs), ALU.mult)

                ubank = [3, 4, 5, 6]

                def back1(i):
                    hd, dr, sg, first, last = items[i]
                    d = DB[i % 2]
                    Kdv, Vtv = d["Kd"], d["V"]
                    if first:
                        k.memset("pool", SBs(0), 0.0)
                        k.memset("dve", S32[0], 0.0)
                        st["sidx"] = 0
                    else:
                        k.cp("pool", SBs(0), SBs(16))
                    for tb in range(4):
                        for r in range(4):
                            k.mm(ps(ubank[r])[:, tb * 128:(tb + 1) * 128], Kdv[r * 32:(r + 1) * 32, tb, :],
                                 Vtv[r * 32:(r + 1) * 32, tb, :], tile_position=(r * 32, 0))

                def back2(i):
                    hd, dr, sg, first, last = items[i]
                    d = DB[i % 2]
                    E1 = d["E1"]
                    lastoff = LCH - 1 if dr == 0 else 0
                    order = list(range(16)) if dr == 0 else list(range(15, -1, -1))
                    slot_of = {}
                    ops = []
                    for n_, ch in enumerate(order):
                        slot_of[ch] = n_

                        def step(n_=n_, ch=ch):
                            tb, r = ch // 4, ch % 4
                            ecol = E1[:, ch * LCH + lastoff:ch * LCH + lastoff + 1]
                            sidx = st["sidx"]
                            k.stt(S32[(sidx + 1) % NS32], S32[sidx % NS32], ecol, ps(ubank[r])[:, tb * 128:(tb + 1) * 128],
                                  ALU.mult, ALU.add)
                            st["sidx"] = sidx + 1
                            k.cp("act" if n_ % 2 == 0 else "pool", SBs(n_ + 1), S32[(sidx + 1) % NS32])
                        ops.append(step)
                    return ops, slot_of

                def back3(i, slot_of):
                    hd, dr, sg, first, last = items[i]
                    wqf, wbi, wgg = hw[hd]
                    d = DB[i % 2]
                    Vtv, STv, Abf = d["V"], d["ST"], d["A"]
                    for tb in range(4):
                        k.mm(ps(7)[:, tb * 128:(tb + 1) * 128], Vtv[:, tb, :], STv[:, tb, :], start=True, stop=False)
                        for r in range(4):
                            ch = tb * 4 + r
                            k.mm(ps(7)[:, ch * LCH:(ch + 1) * LCH], SBs(slot_of[ch]),
                                 Abf[:, ch * LCH:(ch + 1) * LCH], start=False, stop=(r == 3))
                    oseg = OACC[:, sg * 512:(sg + 1) * 512]
                    if dr == 0:
                        k.cp("act", oseg, ps(7))
                    else:
                        k.tt("dve", oseg, oseg, ps(7), ALU.add)
                    if last:
                        SQh, RSh, SGh, TYh = sgm, lgf, xtra, bb
                        for tt in range(NT):
                            osg = OACC[:, tt * 512:(tt + 1) * 512]
                            sqb = Tl(SQh.ap.bitcast(BF)[:, 0:512], SQh.bufs)
                            k.act(sqb, osg, AF.Square)
                            k.mm(ps(7), ONBt, sqb)
                            k.act(RSh, ps(7), AF.Sqrt, bias=EPST, scale=1.0 / 128.0)
                            k.recip(RSh, RSh)
                            proj_fm(ps(3), wgg, 0, tt)
                            k.act(SGh, ps(3), AF.Silu)
                            k.stt(TYh, osg, HNG[:, l:l + 1], RSh, ALU.mult, ALU.mult)
                            k.tt("dve", Yt(8 + hd, tt), TYh, SGh, ALU.mult)

                nit_ = len(items)
                front1(0)
                for f_ in front2(0):
                    f_()
                if nit_ > 1:
                    front1(1)
                front3(0)
                for i in range(nit_):
                    fops, bops, slot_of = [], [], None
                    back1(i)
                    if i + 1 < nit_:
                        fops = front2(i + 1)
                    bops, slot_of = back2(i)
                    na, nb = len(fops), len(bops)
                    ia = ib = 0
                    did_f1 = False
                    while ia < na or ib < nb:
                        if ib >= nb or (ia < na and ia * max(nb, 1) <= ib * max(na, 1)):
                            fops[ia]()
                            ia += 1
                            if ia == 3 and i + 2 < nit_:
                                front1(i + 2)
                                did_f1 = True
                        else:
                            bops[ib]()
                            ib += 1
                    back3(i, slot_of)
                    if i + 1 < nit_:
                        front3(i + 1)

                if debug and l == layers[-1] and s == 0:
                    k.dma("sp", Tl(dbg_d, [Buf()]), Tl(Y[:, :], [b for row in bY for b in row]))

                for th in range(NHALF):
                    AR.reset()
                    Mh = tmp(NJ * HT * 2, BF)
                    Mv = Tl(Mh.ap.rearrange("p (j t) -> p j t", t=HT), Mh.bufs)
                    SG = [tmp(2048, F32), tmp(2048, F32)]
                    AC = [tmp(2048, F32), tmp(2048, F32)]
                    T2 = tmp(2048, F32)
                    for j in range(NJ):
                        for m in range(3):
                            wi = wslot()
                            wgv = wload(wi, wl, 8, C_GATE + m * D + j * 128, 128)
                            wuv = wload(wi, w_up_d[m][l], 4, j * 128, 128, off=1024)
                            for t2 in range(HNT):
                                tt = th * HNT + t2
                                ac = AC[t2 % 2]
                                gb, ub = (2 * (m * HNT + t2)) % 6, (2 * (m * HNT + t2) + 1) % 6
                                proj_fm(ps(gb), wgv, 0, tt)
                                sgt = SG[t2 % 2]
                                k.act(sgt, ps(gb), AF.Sigmoid)
                                for kc in range(4):
                                    k.mm(ps(ub), wuv[:, kc, 0:128], Yt(m * 4 + kc, tt),
                                         start=(kc == 0), stop=(kc == 3))
                                if m == 0:
                                    k.tt("dve", ac, sgt, ps(ub), ALU.mult)
                                elif m == 1:
                                    k.tt("dve", T2, sgt, ps(ub), ALU.mult)
                                    k.tt("dve", ac, ac, T2, ALU.add)
                                else:
                                    k.tt("dve", T2, sgt, ps(ub), ALU.mult)
                                    k.tt("dve", Mv[:, j, t2 * 512:(t2 + 1) * 512], ac, T2, ALU.add)
                                ndrain(1)
                    for j in range(NJ):
                        wo = wslot()
                        wov = wload(wo, w_out_d[l], 8, j * 128, 128)
                        for t2 in range(HNT):
                            tt = th * HNT + t2
                            pb = 6 + t2 % 2
                            proj_fm(ps(pb), wov, 0, tt, rhs_fn=lambda kc: Mv[:, kc, t2 * 512:(t2 + 1) * 512])
                            k.tt("dve", Xt(j, tt), Xt(j, tt), ps(pb), ALU.add)
                    assert AR.off <= NOFF
                    ndrain(len(npend))
                    npend.extend(rmsnorm_ops(G2, l, list(range(th * HNT, (th + 1) * HNT))))

                for th in range(NHALF):
                    AR.reset()
                    SL = [tmp(2048, F32), tmp(2048, F32)]
                    HIDv = Tl(Y[:, 0:NHC * HT].rearrange("p (c t) -> p c t", t=HT), [b for row in bY for b in row])
                    for hc in range(NHC):
                        wf = wslot()
                        wfv = wview(wf, 8, 256)
                        wload(wf, w_fi_d[l], 8, hc * 128, 128, 0, 256)
                        wload(wf, w_fi_d[l], 8, FFN + hc * 128, 128, 128, 256)
                        for t2 in range(HNT):
                            tt = th * HNT + t2
                            gb, ub = (2 * t2) % 4, (2 * t2 + 1) % 4
                            proj_fm(ps(gb), wfv, 0, tt)
                            proj_fm(ps(ub), wfv, 128, tt)
                            sl = SL[t2 % 2]
                            k.act(sl, ps(gb), AF.Silu)
                            k.tt("dve", HIDv[:, hc, t2 * 512:(t2 + 1) * 512], sl, ps(ub), ALU.mult)
                            ndrain(1)
                    ndrain(len(npend))
                    for j in range(NJ):
                        wo1 = wload(wslot(), w_fo_d[l], 11, j * 128, 128, k0=0)
                        wo2 = wload(wslot(), w_fo_d[l], 11, j * 128, 128, k0=11)
                        for t2 in range(HNT):
                            tt = th * HNT + t2
                            pb = 4 + (j * HNT + t2) % 4
                            for kc in range(NHC):
                                wsrc = wo1 if kc < 11 else wo2
                                k.mm(ps(pb), wsrc[:, kc % 11, 0:128], HIDv[:, kc, t2 * 512:(t2 + 1) * 512],
                                     start=(kc == 0), stop=(kc == NHC - 1))
                            k.tt("dve", Xt(j, tt), Xt(j, tt), ps(pb), ALU.add)
                    if th == 0 and NHALF == 2 and l != layers[-1]:
                        npend.extend(rmsnorm_ops(G1, l + 1, list(range(0, HNT))))

            AR.reset()
            if (final if final is not None else layers[-1] == DEPTH - 1):
                SQ = [tmp(1024, BF), tmp(1024, BF)]
                RS = [tmp(2048, F32), tmp(2048, F32)]
                for tt in range(NT):
                    pb = 6 + (tt % 2)
                    for j in range(NJ):
                        sq = SQ[j % 2]
                        k.act(sq, Xt(j, tt), AF.Square)
                        k.mm(ps(pb), ONBt, sq, start=(j == 0), stop=(j == NJ - 1))
                    rs = RS[tt % 2]
                    k.act(rs, ps(pb), AF.Sqrt, bias=EPST, scale=1.0 / D)
                    k.recip(rs, rs)
                    for j in range(NJ):
                        k.stt(Xt(j, tt), Xt(j, tt), GF[:, j:j + 1], rs, ALU.mult, ALU.mult)
            OS = [tmp(4096, F32), tmp(4096, F32)]
            n = 0
            for tt in range(NT):
                for tb in range(4):
                    os_ = OS[n % 2]
                    n += 1
                    pbs = (0, 1) if tb % 2 == 0 else (2, 3)
                    for j in range(NJ):
                        k.tr(ps(pbs[j // 4])[:, (j % 4) * 128:(j % 4 + 1) * 128], Xt(j, tt)[:, tb * 128:(tb + 1) * 128], IDFt)
                    k.cp("act", os_[:, 0:512], ps(pbs[0]))
                    k.cp("dve", os_[:, 512:1024], ps(pbs[1]))
                    t0 = tt * 512 + tb * 128
                    ob = Buf()
                    out_bufs.append(ob)
                    k.dma("sp", Tl(y_d[s, t0:t0 + 128, :], [ob]), os_)
        P.op("sp", None, reads=[Tl(None, out_bufs)], writes=[])

        with nc.Block() as block:
            engs = {}

            @block.tensor
            def _(e):
                engs["pe"] = e
                P_emit_one(P, "pe", e, sems, dma_sems)

            @block.vector
            def _(e):
                P_emit_one(P, "dve", e, sems, dma_sems)

            @block.scalar
            def _(e):
                P_emit_one(P, "act", e, sems, dma_sems)

            @block.gpsimd
            def _(e):
                P_emit_one(P, "pool", e, sems, dma_sems)

            @block.sync
            def _(e):
                P_emit_one(P, "sp", e, sems, dma_sems)
    return nc


_prepared = set()


def P_emit_one(P, ename, eng, sems, dma_sems):
    if id(P) not in _prepared:
        _prepared.add(id(P))
        for e in ENGS:
            for o in P.ops[e]:
                for d in o.deps:
                    d.need_inc = True
        for e in ENGS:
            cnt = 0
            kk = 0
            for o in P.ops[e]:
                if o.dma:
                    pool = dma_sems[e]
                    o.sem = pool[kk % len(pool)]
                    o.semval = 16 * (kk // len(pool) + 1)
                    kk += 1
                elif o.need_inc:
                    cnt += 1
                    o.sem = sems[e]
                    o.semval = cnt
    seen = {}
    for o in P.ops[ename]:
        waits = {}
        for d in o.deps:
            sid = id(d.sem)
            if sid not in waits or waits[sid][1] < d.semval:
                waits[sid] = (d.sem, d.semval)
        for sid, (s_, v) in waits.items():
            if seen.get(sid, 0) >= v:
                continue
            seen[sid] = v
            eng.wait_ge(s_, v)
        if o.fn is None:
            continue
        ins = o.fn(eng)
        if o.dma:
            ins.then_inc(o.sem, 16)
        elif o.need_inc:
            ins.then_inc(o.sem, 1)


def host_consts(S):
    NKC = S // 128
    identf = np.eye(128, dtype=np.float32)
    tri = np.zeros((128, 256), np.float32)
    for s_ in range(128):
        for t in range(128):
            if s_ // LCH == t // LCH:
                if s_ <= t:
                    tri[s_, t] = 1.0
                if s_ >= t:
                    tri[s_, 128 + t] = 1.0
    smask = np.ones((128, 512), np.float32)
    smask[:, ::LCH] = 0.0
    base = (NKC - 1) * 128
    TBW = S + base
    m = np.arange(TBW)[None, :]
    p = np.arange(128)[:, None]
    alibi = np.abs(p - m + base).astype(np.float32)
    return {"identf": identf, "tri": tri, "smask": smask, "alibi": alibi}


def host_params(inp):
    f = lambda a: np.ascontiguousarray(np.asarray(a, dtype=np.float32))
    L = DEPTH
    out = {}
    out["g1T"] = f(np.asarray(inp["norm1_g"]).reshape(L, 8, 128).transpose(2, 0, 1).reshape(128, L * 8))
    out["g2T"] = f(np.asarray(inp["norm2_g"]).reshape(L, 8, 128).transpose(2, 0, 1).reshape(128, L * 8))
    out["gfT"] = f(np.asarray(inp["final_norm_g"]).reshape(8, 128).T)
    out["pscT"] = f(np.asarray(inp["pool_scale"]).reshape(L, 4, 128).transpose(2, 0, 1).reshape(128, L * 4))
    out["dng"] = f(np.broadcast_to(np.asarray(inp["diff_norm_g"]).reshape(1, L * 128), (128, L * 128)))
    out["hngT"] = f(np.asarray(inp["hgrn_norm_g"]).T)
    lamv = np.concatenate([np.asarray(inp[n]).reshape(1, L * 64) for n in ("lam_q1", "lam_k1", "lam_q2", "lam_k2")], axis=1)
    out["lamv"] = f(np.broadcast_to(lamv, (128, 4 * L * 64)))
    out["lbT"] = f(np.asarray(inp["hgrn_lb"]).reshape(2, L, 4, 128).transpose(3, 0, 2, 1).reshape(128, 2 * 4 * L))
    for n in ("w_in", "pool_w", "w_up_pool", "w_up_attn", "w_up_rec", "w_out", "w_ffn_in", "w_ffn_out"):
        out[n] = f(inp[n])
    return out


_nc_cache = {}


def kernel(**inputs):
    x = np.asarray(inputs["x"], dtype=np.float32)
    B, S, _ = x.shape
    ncores = 8
    nseq = B // ncores
    shared = host_params(inputs)
    shared.update(host_consts(S))
    nc = build(S=S, NSEQ=nseq)
    in_maps = []
    for c in range(ncores):
        m = dict(shared)
        m["x"] = np.ascontiguousarray(x[c * nseq:(c + 1) * nseq])
        in_maps.append(m)
    res = run_bass_kernel_spmd(nc, in_maps, core_ids=list(range(ncores)))
    return np.concatenate([np.asarray(r["y"], dtype=np.float32) for r in res.results], axis=0)
```

```python
import math
from contextlib import ExitStack
import numpy as np
import concourse.bass as bass
import concourse.mybir as mybir
from concourse.bass_utils import run_bass_kernel_spmd

F32 = mybir.dt.float32
BF = mybir.dt.bfloat16
F16 = mybir.dt.float16
AF = mybir.ActivationFunctionType
ALU = mybir.AluOpType
AX = mybir.AxisListType

D = 1024
NJ = 8
DEPTH = 4
FFN = 2816
NHC = 22
IN_COLS = 7680
EPS = 1e-6
C_POOL, C_AQ, C_AK, C_AV, C_RQ, C_RF, C_RB, C_RI, C_RG, C_GATE = 0, 512, 1024, 1536, 2048, 2560, 3072, 3584, 4096, 4608
SLOPES = [2.0 ** (-8.0 * (h + 1) / 4) for h in range(4)]
LCH = 32


class Buf:
    __slots__ = ("name", "lw", "rd")

    def __init__(self, name=""):
        self.name = name
        self.lw = []
        self.rd = {}


class Tl:
    __slots__ = ("ap", "bufs")

    def __init__(self, ap, bufs):
        self.ap = ap
        self.bufs = tuple(bufs)

    def __getitem__(self, idx):
        return Tl(self.ap[idx], self.bufs)


class Op:
    __slots__ = ("eng", "fn", "deps", "dma", "need_inc", "ticket", "sem", "semval", "idx")


ENGS = ("pe", "dve", "act", "pool", "sp")


class Prog:
    def __init__(self, nc, ndma_sems=24):
        self.nc = nc
        self.ops = {e: [] for e in ENGS}
        self.n = 0
        self.ndma = ndma_sems
        self.dma_hist = {e: [] for e in ENGS}

    def op(self, eng, fn, reads=(), writes=(), dma=False):
        o = Op()
        o.eng, o.fn, o.dma, o.need_inc, o.ticket, o.sem, o.semval = eng, fn, dma, False, 0, None, 0
        o.idx = self.n
        self.n += 1
        deps = {}
        rb, wb = [], []
        for t in reads:
            if t is not None:
                rb.extend(t.bufs)
        for t in writes:
            wb.extend(t.bufs)
        for b in rb:
            for w_ in b.lw:
                deps[id(w_)] = w_
        for b in wb:
            for w_ in b.lw:
                deps[id(w_)] = w_
            for r in b.rd.values():
                deps[id(r)] = r
        key = ("dma", o.idx) if dma else eng
        for b in rb:
            b.rd[key] = o
        for b in wb:
            b.lw = [o]
            b.rd = {}
        if dma:
            h = self.dma_hist[eng]
            if len(h) >= self.ndma:
                p = h[len(h) - self.ndma]
                deps[id(p)] = p
            h.append(o)
        dl = []
        for d in deps.values():
            if d is o:
                continue
            if d.eng == "pe" and eng == "pe" and not d.dma and not dma:
                continue
            dl.append(d)
        o.deps = dl
        self.ops[eng].append(o)
        return o

    def emit(self, block_engines, sems, dma_sems):
        nc = self.nc
        for e in ENGS:
            for o in self.ops[e]:
                for d in o.deps:
                    d.need_inc = True
        for e in ENGS:
            cnt = 0
            k = 0
            for o in self.ops[e]:
                if o.dma:
                    pool = dma_sems[e]
                    o.sem = pool[k % len(pool)]
                    o.semval = 16 * (k // len(pool) + 1)
                    k += 1
                elif o.need_inc:
                    cnt += 1
                    o.ticket = cnt
                    o.sem = sems[e]
                    o.semval = cnt
        for e in ENGS:
            eng = block_engines[e]
            seen = {}
            for o in self.ops[e]:
                waits = {}
                for d in o.deps:
                    sid = id(d.sem)
                    if sid not in waits or waits[sid][1] < d.semval:
                        waits[sid] = (d.sem, d.semval)
                for sid, (s, v) in waits.items():
                    if seen.get(sid, 0) >= v:
                        continue
                    seen[sid] = v
                    eng.wait_ge(s, v)
                if o.fn is None:
                    continue
                ins = o.fn(eng)
                if o.dma:
                    ins.then_inc(o.sem, 16)
                elif o.need_inc:
                    ins.then_inc(o.sem, 1)


class K:
    def __init__(self, P):
        self.P = P

    @staticmethod
    def _a(x):
        return x.ap if isinstance(x, Tl) else x

    def mm(self, out, lhsT, rhs, start=True, stop=True, **kw):
        self.P.op("pe", lambda e: e.matmul(out.ap, lhsT.ap, rhs.ap, start=start, stop=stop, **kw),
                  reads=[lhsT, rhs], writes=[out])

    def tr(self, out, in_, ident):
        self.P.op("pe", lambda e: e.transpose(out.ap, in_.ap, ident.ap), reads=[in_, ident], writes=[out])

    def act(self, out, in_, func, bias=None, scale=None, accum=None):
        kw = {}
        rd = [in_]
        if bias is not None:
            kw["bias"] = self._a(bias)
            if isinstance(bias, Tl):
                rd.append(bias)
        if scale is not None:
            kw["scale"] = self._a(scale)
            if isinstance(scale, Tl):
                rd.append(scale)
        wr = [out]
        if accum is not None:
            kw["accum_out"] = accum.ap
            wr.append(accum)
        self.P.op("act", lambda e: e.activation(out.ap, in_.ap, func, **kw), reads=rd, writes=wr)

    def ts(self, eng, out, in0, s1, s2, op0, op1=None):
        rd = [in0] + [s for s in (s1, s2) if isinstance(s, Tl)]
        a1, a2 = self._a(s1), self._a(s2)
        if op1 is None:
            f = lambda e: e.tensor_scalar(out.ap, in0.ap, a1, a2, op0)
        else:
            f = lambda e: e.tensor_scalar(out.ap, in0.ap, a1, a2, op0, op1)
        self.P.op(eng, f, reads=rd, writes=[out])

    def stt(self, out, in0, scalar, in1, op0, op1):
        rd = [in0, in1] + ([scalar] if isinstance(scalar, Tl) else [])
        a = self._a(scalar)
        self.P.op("dve", lambda e: e.scalar_tensor_tensor(out.ap, in0.ap, a, in1.ap, op0, op1), reads=rd, writes=[out])

    def tt(self, eng, out, in0, in1, op):
        self.P.op(eng, lambda e: e.tensor_tensor(out.ap, in0.ap, in1.ap, op), reads=[in0, in1], writes=[out])

    def cp(self, eng, out, in_):
        if eng == "act":
            self.P.op("act", lambda e: e.copy(out.ap, in_.ap), reads=[in_], writes=[out])
        else:
            self.P.op(eng, lambda e: e.tensor_copy(out.ap, in_.ap), reads=[in_], writes=[out])

    def recip(self, out, in_):
        self.P.op("dve", lambda e: e.reciprocal(out.ap, in_.ap), reads=[in_], writes=[out])

    def memset(self, eng, out, val):
        self.P.op(eng, lambda e: e.memset(out.ap, val), reads=[], writes=[out])

    def reduce(self, out, in_, op, axis=AX.X):
        self.P.op("dve", lambda e: e.tensor_reduce(out.ap, in_.ap, axis, op), reads=[in_], writes=[out])

    def scan(self, out, d0, d1, init, op0, op1):
        self.P.op("dve", lambda e: e.tensor_tensor_scan(out.ap, d0.ap, d1.ap, init, op0, op1), reads=[d0, d1], writes=[out])

    def dma(self, eng, out, in_):
        self.P.op(eng, lambda e: e.dma_start(out=out.ap, in_=in_.ap), reads=[in_], writes=[out], dma=True)


def mkap(t, offset, pairs):
    a = t.ap if isinstance(t, Tl) else t
    return bass.AP(tensor=a.tensor, offset=offset, ap=[list(p) for p in pairs])


def build(S=2048, NL=DEPTH, NSEQ=2, layers=None, debug=False, final=None):
    NT = S // 512
    NKC = S // 128
    NHALF = 2 if S >= 1024 else 1
    HT = S // NHALF
    HNT = HT // 512
    if layers is None:
        layers = list(range(NL))
    nc = bass.Bass("TRN2", target_bir_lowering=False)
    P = Prog(nc)
    k = K(P)
    dram = {}

    def din(name, shape, dt=F32):
        dram[name] = nc.dram_tensor(name, list(shape), dt, kind="ExternalInput").ap()
        return dram[name]

    x_d = din("x", [NSEQ, S, D])
    w_in_d = din("w_in", [DEPTH, D, IN_COLS])
    pool_w_d = din("pool_w", [DEPTH, 4, 128, 128])
    w_up_d = [din("w_up_pool", [DEPTH, 512, D]), din("w_up_attn", [DEPTH, 512, D]), din("w_up_rec", [DEPTH, 512, D])]
    w_out_d = din("w_out", [DEPTH, D, D])
    w_fi_d = din("w_ffn_in", [DEPTH, D, 2 * FFN])
    w_fo_d = din("w_ffn_out", [DEPTH, FFN, D])
    g1_d = din("g1T", [128, DEPTH * 8])
    g2_d = din("g2T", [128, DEPTH * 8])
    gf_d = din("gfT", [128, 8])
    psc_d = din("pscT", [128, DEPTH * 4])
    dng_d = din("dng", [128, DEPTH * 128])
    hng_d = din("hngT", [128, DEPTH])
    lamv_d = din("lamv", [128, 4 * DEPTH * 64])
    lb_d = din("lbT", [128, 2 * 4 * DEPTH])
    identf_d = din("identf", [128, 128])
    tri_d = din("tri", [128, 256])
    smask_d = din("smask", [128, 512])
    TBW = S + (NKC - 1) * 128
    alibi_d = din("alibi", [128, TBW])
    y_d = nc.dram_tensor("y", [NSEQ, S, D], F32, kind="ExternalOutput").ap()
    if debug:
        dbg_d = nc.dram_tensor("dbg", [128, 12 * S], BF, kind="ExternalOutput").ap()

    es = ExitStack()
    with es:
        def sb(name, shape, dt):
            return es.enter_context(nc.sbuf_tensor(name, list(shape), dt))

        X = sb("X", [128, NJ, S], F32)
        H = sb("H", [128, NJ, S], BF)
        Y = sb("Y", [128, 12 * S], BF)
        NWS = 4
        WS = [sb(f"WS{i}", [128, 2048], BF) for i in range(NWS)]
        CF = sb("CF", [128, 1024], F32)
        IDF = sb("IDF", [128, 128], F32)
        IDB = sb("IDB", [128, 128], BF)
        ONB = sb("ONB", [128, 128], BF)
        ZRB = sb("ZRB", [128, 512], BF)
        TRI = sb("TRI", [128, 256], BF)
        SMK = sb("SMK", [128, 512], F32)
        TMP = sb("TMP", [128, 38 * 1024 // 4], F32)
        PS = [es.enter_context(nc.psum_tensor(f"ps{i}", [128, 512], F32)) for i in range(8)]
        sems = {e: es.enter_context(nc.semaphore(f"s_{e}")) for e in ENGS}
        dma_sems = {e: [es.enter_context(nc.semaphore(f"d_{e}{i}")) for i in range(P.ndma)] for e in ("sp", "pool")}
        dma_sems["act"] = dma_sems["sp"]
        dma_sems["pe"] = dma_sems["sp"]
        dma_sems["dve"] = dma_sems["sp"]

        bX = [[Buf() for _ in range(NT)] for _ in range(NJ)]
        bH = [[Buf() for _ in range(NT)] for _ in range(NJ)]
        bY = [[Buf() for _ in range(NT)] for _ in range(12)]
        bPS = [Buf(f"ps{i}") for i in range(8)]
        bWS = [Buf() for _ in range(NWS)]
        bC = Buf("consts")

        def Xt(j, tt):
            return Tl(X[:, j, tt * 512:(tt + 1) * 512], [bX[j][tt]])

        def Ht(j, tt):
            return Tl(H[:, j, tt * 512:(tt + 1) * 512], [bH[j][tt]])

        Yv = Y[:, :].rearrange("p (c t) -> p c t", t=S)

        def Yt(c, tt):
            return Tl(Yv[:, c, tt * 512:(tt + 1) * 512], [bY[c][tt]])

        def ps(i):
            return Tl(PS[i][:, :], [bPS[i]])

        def psb(i):
            return Tl(PS[i][:, :].bitcast(BF), [bPS[i]])

        class Arena:
            def __init__(self):
                self.off = 0

            def reset(self):
                self.off = 0

            def get(self, nbytes, dt, shape=None):
                assert self.off % 4 == 0
                n32 = (nbytes + 3) // 4
                assert self.off // 4 + n32 <= 38 * 1024 // 4, "TMP arena overflow"
                a = TMP[:, self.off // 4: self.off // 4 + n32]
                self.off += n32 * 4
                if dt != F32:
                    a = a.bitcast(dt)
                return a

        AR = Arena()
        arena_cur = []
        arena_old = []

        _areset = AR.reset

        def areset():
            arena_old.extend(arena_cur)
            del arena_cur[:]
            if len(arena_old) > 400:
                del arena_old[:len(arena_old) - 400]
            _areset()

        AR.reset = areset

        norm_recs = []

        def tmp_fixed(off, nbytes, dt):
            n32 = (nbytes + 3) // 4
            a = TMP[:, off // 4: off // 4 + n32]
            if dt != F32:
                a = a.bitcast(dt)
            nb = Buf(f"fix{off}")
            norm_recs.append((off, off + n32 * 4, nb))
            return Tl(a, [nb])

        def norm_refresh():
            for (f0, f1, fb) in norm_recs:
                for (a0, a1, ob) in arena_old + arena_cur:
                    if a0 < f1 and f0 < a1:
                        for w_ in ob.lw:
                            if all(w_ is not q for q in fb.lw):
                                fb.lw.append(w_)
                        for kk, r in ob.rd.items():
                            if kk not in fb.rd or fb.rd[kk].idx < r.idx:
                                fb.rd[kk] = r

        def tmp(nbytes, dt):
            off = AR.off
            ap = AR.get(nbytes, dt)
            end = AR.off
            nb = Buf(f"tmp{off}")
            for (a0, a1, ob) in arena_old + norm_recs:
                if a0 < end and off < a1:
                    for w_ in ob.lw:
                        if all(w_ is not q for q in nb.lw):
                            nb.lw.append(w_)
                    for kk, r in ob.rd.items():
                        if kk not in nb.rd or nb.rd[kk].idx < r.idx:
                            nb.rd[kk] = r
            arena_cur.append((off, end, nb))
            return Tl(ap, [nb])

        cfo = [0]

        def cf(n):
            o = cfo[0]
            cfo[0] += n
            assert cfo[0] <= 1024
            return Tl(CF[:, o:o + n], [bC])

        G1, G2, GF = cf(32), cf(32), cf(8)
        PSC, HNG = cf(16), cf(4)
        LAMT, NEGLAM = cf(4), cf(4)
        LBT, LBV, OML = cf(32), cf(32), cf(32)
        EPST = cf(1)
        EPSA = cf(4)
        RCE = cf(64)
        SMALL = Tl(cf(64).ap, [Buf('small')])
        IDFt = Tl(IDF[:, :], [Buf()])
        IDBt = Tl(IDB[:, :], [Buf()])
        ONBt = Tl(ONB[:, :], [Buf()])
        ZRBt = Tl(ZRB[:, :], [Buf()])
        TRIt = Tl(TRI[:, :], [Buf()])
        SMKt = Tl(SMK[:, :], [Buf()])
        ext = lambda ap: Tl(ap, [])

        k.dma("sp", G1, ext(g1_d))
        k.dma("sp", G2, ext(g2_d))
        k.dma("sp", GF, ext(gf_d))
        k.dma("sp", PSC, ext(psc_d))
        k.dma("sp", HNG, ext(hng_d))
        k.dma("sp", LBT, ext(lb_d))
        k.dma("sp", IDFt, ext(identf_d))
        k.dma("sp", SMKt, ext(smask_d))
        k.dma("pool", TRIt, ext(tri_d))
        k.cp("dve", IDBt, IDFt)
        k.memset("dve", ONBt, 1.0)
        k.memset("dve", ZRBt, 0.0)
        k.memset("dve", EPST, EPS)
        lam_init = [0.8 - 0.6 * math.exp(-0.3 * l) for l in range(DEPTH)]
        for l in range(DEPTH):
            k.memset("dve", EPSA[:, l:l + 1], EPS / (1.0 - lam_init[l]) ** 2)
        for g in range(4):
            hw = 2 ** g
            for i in range(hw):
                k.memset("dve", RCE[:, g * 16 + i:g * 16 + i + 1], 1.0 / (i + hw))
            for i in range(hw - 1):
                k.memset("dve", RCE[:, g * 16 + 8 + i:g * 16 + 9 + i], 1.0 / (2 * hw - 1 - i))
        AR.reset()
        LV = tmp(4 * DEPTH * 64 * 4, F32)
        k.dma("sp", LV, ext(lamv_d))
        PR = tmp(2 * DEPTH * 64 * 4, F32)
        n1 = DEPTH * 64
        k.tt("dve", PR[:, 0:n1], LV[:, 0:n1], LV[:, n1:2 * n1], ALU.mult)
        k.tt("dve", PR[:, n1:2 * n1], LV[:, 2 * n1:3 * n1], LV[:, 3 * n1:4 * n1], ALU.mult)
        S12 = SMALL[:, 0:8]
        k.reduce(S12, Tl(PR.ap.rearrange("p (a d) -> p a d", d=64), PR.bufs), ALU.add)
        E12 = SMALL[:, 8:16]
        k.act(E12, S12, AF.Exp)
        k.tt("dve", LAMT, E12[:, 0:4], E12[:, 4:8], ALU.subtract)
        for l in range(DEPTH):
            k.ts("dve", LAMT[:, l:l + 1], LAMT[:, l:l + 1], lam_init[l], None, ALU.add)
        k.ts("dve", NEGLAM, LAMT, -1.0, None, ALU.mult)
        ELB = SMALL[:, 16:48]
        k.act(ELB, LBT, AF.Exp)
        SLB = SMALL[:, 48:56]
        k.reduce(SLB, Tl(ELB.ap.rearrange("p (a l) -> p a l", l=DEPTH), ELB.bufs), ALU.add)
        k.recip(SLB, SLB)
        SMX = LBT
        k.tt("dve", Tl(SMX.ap.rearrange("p (a l) -> p a l", l=DEPTH), SMX.bufs),
             Tl(ELB.ap.rearrange("p (a l) -> p a l", l=DEPTH), ELB.bufs),
             Tl(mkap(SLB, SLB.ap.offset, [SLB.ap.ap[0], [1, 8], [0, DEPTH]]), SLB.bufs), ALU.mult)
        smv = SMX.ap.rearrange("p (a l) -> p a l", l=DEPTH)
        lbv = LBV.ap.rearrange("p (a l) -> p a l", l=DEPTH)
        k.memset("dve", Tl(lbv[:, :, 0:1], LBV.bufs), 0.0)
        k.cp("dve", Tl(lbv[:, :, 1:2], LBV.bufs), Tl(smv[:, :, 1:2], SMX.bufs))
        for l in range(2, DEPTH):
            k.tt("dve", Tl(lbv[:, :, l:l + 1], LBV.bufs), Tl(lbv[:, :, l - 1:l], LBV.bufs), Tl(smv[:, :, l:l + 1], SMX.bufs), ALU.add)
        k.ts("dve", OML, LBV, -1.0, 1.0, ALU.mult, ALU.add)

        ws_i = [0]

        def wslot():
            i = ws_i[0] % NWS
            ws_i[0] += 1
            return i

        def wview(i, nk, ncols, off=0):
            assert off + nk * ncols <= 2048
            return Tl(WS[i][:, off:off + nk * ncols].rearrange("p (k c) -> p k c", c=ncols), [bWS[i]])

        def wload(i, src2d, nk, c0, ncols, dst_c0=0, tot_cols=None, off=0, k0=0):
            tot = tot_cols or ncols
            v = wview(i, nk, tot, off)
            srcv = src2d.rearrange("(k p) c -> p k c", p=128)[:, k0:k0 + nk, c0:c0 + ncols]
            k.dma("pool", v[:, :, dst_c0:dst_c0 + ncols], ext(srcv))
            return v

        def proj_fm(out_ps, w, wc0, tt, nk=8, rhs_fn=None):
            for kc in range(nk):
                rhs = Ht(kc, tt) if rhs_fn is None else rhs_fn(kc)
                k.mm(out_ps, w[:, kc, wc0:wc0 + 128], rhs, start=(kc == 0), stop=(kc == nk - 1))

        NOFF = 32 * 1024
        NSQ = [tmp_fixed(NOFF, 1024, BF), tmp_fixed(NOFF + 1024, 1024, BF)]
        NRS = [tmp_fixed(NOFF + 2048, 2048, F32), tmp_fixed(NOFF + 4096, 2048, F32)]
        npend = []

        def ndrain(n_):
            for _ in range(min(n_, len(npend))):
                npend.pop(0)()

        def rmsnorm_ops(G, l, tts):
            norm_refresh()
            ops = []
            for tt in tts:
                pb = 6 + (tt % 2)
                rs = NRS[tt % 2]
                for j in range(NJ):
                    sq = NSQ[j % 2]

                    def f1(j=j, tt=tt, sq=sq, pb=pb):
                        k.act(sq, Xt(j, tt), AF.Square)
                        k.mm(ps(pb), ONBt, sq, start=(j == 0), stop=(j == NJ - 1))
                    ops.append(f1)

                def f2(rs=rs, pb=pb):
                    k.act(rs, ps(pb), AF.Ln, bias=EPST, scale=1.0 / D)
                    k.act(rs, rs, AF.Exp, scale=-0.5)
                ops.append(f2)
                for j in range(NJ):
                    ops.append(lambda j=j, tt=tt, rs=rs: k.stt(Ht(j, tt), Xt(j, tt), G[:, l * 8 + j:l * 8 + j + 1], rs,
                                                                 ALU.mult, ALU.mult))
            return ops

        out_bufs = []
        for s in range(NSEQ):
            AR.reset()
            XS = [tmp(4096, F32), tmp(4096, F32)]
            n = 0
            for tt in range(NT):
                for tb in range(4):
                    xs = XS[n % 2]
                    n += 1
                    t0 = tt * 512 + tb * 128
                    k.dma("sp", xs, ext(x_d[s, t0:t0 + 128, :]))
                    for j in range(NJ):
                        k.tr(ps(j)[:, tb * 128:(tb + 1) * 128], xs[:, j * 128:(j + 1) * 128], IDFt)
                for j in range(NJ):
                    k.cp("act" if j % 2 else "dve", Xt(j, tt), ps(j))

            for l in layers:
                wl = w_in_d[l]
                AR.reset()
                ndrain(len(npend))
                n1_tts = list(range(NT)) if (l == layers[0] or NHALF == 1) else list(range(HNT, NT))
                for f_ in rmsnorm_ops(G1, l, n1_tts):
                    f_()

                AR.reset()
                WD = S + 24
                U = tmp(WD * 4, F32)
                A_ = tmp(WD * 4, F32)
                B_ = tmp(WD * 4, F32)
                DT = tmp(S * 2, BF)
                PWt = tmp(4 * 128 * 2, BF)
                k.memset("pool", U[:, 0:8], 0.0)
                k.memset("pool", U[:, 8 + S:WD], 0.0)
                pwv = Tl(PWt.ap.rearrange("p (g d) -> p g d", d=128), PWt.bufs)
                k.dma("pool", pwv, ext(pool_w_d[l].rearrange("g c d -> c g d")))
                wps = [wload(wslot(), wl, 8, C_POOL + g * 128, 128) for g in range(4)]
                for g in range(4):
                    w = 2 ** (g + 1)
                    hw = w // 2
                    wp = wps[g]
                    for tt in range(NT):
                        pb = tt % 4
                        proj_fm(ps(pb), wp, 0, tt)
                        k.cp("act", U[:, 8 + tt * 512:8 + (tt + 1) * 512], ps(pb))
                    src, dst = U, A_
                    step = 1
                    while step < w:
                        nn = WD - (2 * step - 1)
                        k.tt("pool", dst[:, 0:nn], src[:, 0:nn], src[:, step:step + nn], ALU.add)
                        src, dst = dst, (B_ if dst is A_ else A_)
                        step *= 2
                    sw = src
                    o0 = 8 - hw
                    k.stt(DT[:, 0:S], sw[:, o0:o0 + S], 1.0 / w, U[:, 8:8 + S], ALU.mult, ALU.subtract)
                    ET = SMALL[:, 0:8]
                    k.tt("dve", ET[:, 0:hw], sw[:, o0:o0 + hw], RCE[:, g * 16:g * 16 + hw], ALU.mult)
                    k.tt("dve", DT[:, 0:hw], ET[:, 0:hw], U[:, 8:8 + hw], ALU.subtract)
                    if hw > 1:
                        t0 = S - hw + 1
                        ne = hw - 1
                        k.tt("dve", ET[:, 0:ne], sw[:, o0 + t0:o0 + t0 + ne], RCE[:, g * 16 + 8:g * 16 + 8 + ne], ALU.mult)
                        k.tt("dve", DT[:, t0:t0 + ne], ET[:, 0:ne], U[:, 8 + t0:8 + t0 + ne], ALU.subtract)
                    for tt in range(NT):
                        pb = 4 + tt % 2
                        k.mm(ps(pb), pwv[:, g, :], DT[:, tt * 512:(tt + 1) * 512])
                        k.act(Yt(g, tt), ps(pb), AF.Identity, scale=PSC[:, l * 4 + g:l * 4 + g + 1])

                AR.reset()
                KT = tmp(S * 2, BF)
                VA = tmp(NKC * 132 * 2, BF)
                VAv = Tl(VA.ap.rearrange("p (c e) -> p c e", e=132), VA.bufs)
                TBL = tmp(TBW * 2, F16)
                NR = 3
                TM = [tmp(2048, F32) for _ in range(NR)]
                PT = [tmp(1024, BF) for _ in range(NR)]
                QA = [[tmp(1024, BF), tmp(1024, BF)] for _ in range(2)]
                OS = tmp(1032 * 4, F32)
                OBq = [tmp(512, F32) for _ in range(4)]
                T1q = [tmp(512, F32) for _ in range(2)] * 2
                YTq = [tmp(256, BF) for _ in range(4)]
                SMq = [tmp(32, F32) for _ in range(4)]
                DNG = tmp(512, F32)
                NH5 = tmp(4, F32)
                NH1 = tmp(4, F32)
                k.dma("pool", TBL, ext(alibi_d))
                k.dma("sp", DNG, ext(dng_d[:, l * 128:(l + 1) * 128]))
                k.memset("dve", VAv[:, :, 128:129], 1.0)
                for p_ in range(2):
                    k.memset("pool", QA[p_][0][64:128, :], 0.0)
                    k.memset("pool", QA[p_][1][0:64, :], 0.0)
                k.memset("pool", NH5, -0.5)
                k.memset("pool", NH1, -1.0)
                base = (NKC - 1) * 128
                pend = []

                def drain(nops):
                    for _ in range(min(nops, len(pend))):
                        pend.pop(0)()

                def oacc(c, qb):
                    idx = c * 4 + qb
                    return ps(5 + idx // 3)[:, (idx % 3) * 129:(idx % 3) * 129 + 129]

                def osv(c, qb):
                    idx = c * 4 + qb
                    return OS[:, idx * 129:idx * 129 + 129]

                qcount = [0]

                def qproj_ops(wv_, qt, par):
                    ops = []
                    for kc in range(8):
                        ops.append(lambda kc=kc: k.mm(ps(3), wv_[:, kc, 0:128], Ht(kc, qt), start=(kc == 0), stop=(kc == 7)))
                    ops.append(lambda: k.cp("act", QA[par][0][0:64, :], ps(3)[0:64, :]))
                    ops.append(lambda: k.cp("act", QA[par][1][64:128, :], ps(3)[64:128, :]))
                    return ops

                def epilogue_ops(hd, qt):
                    ops = []
                    sc_ = 1.0 / (128.0 * (1.0 - lam_init[l]) ** 2)
                    stages = [
                        lambda qb: k.tt("pool", SMq[qb][:, 0:1], osv(0, qb)[:, 128:129], NH1, ALU.pow),
                        lambda qb: k.tt("pool", SMq[qb][:, 1:2], osv(1, qb)[:, 128:129], NH1, ALU.pow),
                        lambda qb: k.ts("pool", SMq[qb][:, 2:3], SMq[qb][:, 1:2], NEGLAM[:, l:l + 1], None, ALU.mult),
                        lambda qb: k.ts("pool", T1q[qb], osv(1, qb)[:, 0:128], SMq[qb][:, 2:3], None, ALU.mult),
                        lambda qb: k.stt(OBq[qb], osv(0, qb)[:, 0:128], SMq[qb][:, 0:1], T1q[qb], ALU.mult, ALU.add),
                        lambda qb: k.act(T1q[qb], OBq[qb], AF.Square, accum=SMq[qb][:, 3:4]),
                        lambda qb: k.ts("dve", SMq[qb][:, 4:5], SMq[qb][:, 3:4], sc_, EPSA[:, l:l + 1], ALU.mult, ALU.add),
                        lambda qb: k.tt("pool", SMq[qb][:, 5:6], SMq[qb][:, 4:5], NH5, ALU.pow),
                        lambda qb: k.stt(Tl(YTq[qb].ap, YTq[qb].bufs), OBq[qb], SMq[qb][:, 5:6], DNG, ALU.mult, ALU.mult),
                        lambda qb: k.tr(psb(4)[:, qb * 128:(qb + 1) * 128], YTq[qb], IDBt),
                    ]
                    order = []
                    for si in (0, 1, 2):
                        order += [(si, qb) for qb in range(4)]
                    order += [(3, 0), (3, 1), (4, 0), (4, 1), (3, 2), (3, 3), (4, 2), (4, 3)]
                    for si in range(5, len(stages)):
                        order += [(si, qb) for qb in range(4)]
                    for si, qb in order:
                        ops.append(lambda st_=stages[si], qb=qb: st_(qb))
                    ops.append(lambda: k.cp("act", Yt(4 + hd, qt), psb(4)[:, 0:512]))
                    return ops

                aw = {}

                def attn_weights(hd):
                    wi = wslot()
                    wv_ = wview(wi, 8, 256)
                    wload(wi, wl, 8, C_AQ + hd * 128, 128, 0, 256)
                    wload(wi, wl, 8, C_AK + hd * 128, 128, 128, 256)
                    wvv = wload(wslot(), wl, 8, C_AV + hd * 128, 128)
                    aw[hd] = (wv_, wvv)

                attn_weights(0)
                for hd in range(4):
                    wv_, wvv = aw[hd]
                    for tt in range(NT):
                        proj_fm(ps(tt % 3), wv_, 128, tt)
                        k.cp("dve", KT[:, tt * 512:(tt + 1) * 512], ps(tt % 3))
                        drain(6)
                    for tt in range(NT):
                        pb = tt % 3
                        for tb in range(4):
                            for kc in range(8):
                                k.mm(ps(pb)[:, tb * 128:(tb + 1) * 128], Ht(kc, tt)[:, tb * 128:(tb + 1) * 128],
                                     wvv[:, kc, 0:128], start=(kc == 0), stop=(kc == 7))
                        k.cp("act", VAv[:, tt * 4:(tt + 1) * 4, 0:128],
                             Tl(ps(pb).ap.rearrange("p (b e) -> p b e", e=128), ps(pb).bufs))
                        drain(6)
                    drain(len(pend))
                    nslope8 = -8.0 * SLOPES[hd]
                    par0 = qcount[0] % 2
                    for f_ in qproj_ops(wv_, 0, par0):
                        f_()
                    for qt in range(NT):
                        par = qcount[0] % 2
                        qcount[0] += 1
                        Qc = QA[par]
                        if qt + 1 < NT:
                            pend.extend(qproj_ops(wv_, qt + 1, (par + 1) % 2))
                        elif hd + 1 < 4:
                            attn_weights(hd + 1)
                        for bnk in range(5, 8):
                            k.mm(ps(bnk), ZRBt[:, 0:128], ZRBt, start=True, stop=True, skip_group_check=True)
                        its = [(kc, c) for kc in range(NKC) for c in range(2)]
                        nit = len(its)
                        SK = 2
                        for i in range(nit + SK):
                            if i < nit:
                                kc, c = its[i]
                                r = i % NR
                                sb_ = ps(i % 3)
                                k.mm(sb_, KT[:, kc * 128:(kc + 1) * 128], Qc[c])
                                off = qt * 512 - kc * 128 + base
                                k.stt(TM[r], TBL[:, off:off + 512], nslope8, sb_, ALU.mult, ALU.add)
                                k.act(PT[r], TM[r], AF.Exp, scale=0.125)
                            if i >= SK:
                                kc, c = its[i - SK]
                                r = (i - SK) % NR
                                for qb in range(4):
                                    k.mm(oacc(c, qb), PT[r][:, qb * 128:(qb + 1) * 128], VAv[:, kc, 0:129],
                                         start=False, stop=(kc == NKC - 1), skip_group_check=True)
                            rem = nit + SK - i
                            drain((len(pend) + rem - 1) // rem)
                        k.cp("act", OS[:, 0:387], ps(5)[:, 0:387])
                        k.cp("act", OS[:, 387:774], ps(6)[:, 0:387])
                        k.cp("act", OS[:, 774:1032], ps(7)[:, 0:258])
                        pend.extend(epilogue_ops(hd, qt))
                drain(len(pend))

                AR.reset()
                sgm, lgf, bb, xtra = [tmp(2048, F32) for _ in range(4)]
                key, Einv = sgm, bb
                Bbf = tmp(1024, BF)
                KdT = tmp(1024, BF)
                v4 = lambda t: Tl(t.ap.rearrange("p (b e) -> p b e", e=128), t.bufs)
                DB = []
                for _ in range(2):
                    DB.append(dict(A=tmp(1024, BF), V=v4(tmp(1024, BF)), ST=v4(tmp(1024, BF)), Kd=v4(tmp(1024, BF)),
                                   E1=tmp(2048, F32)))
                SBFr = tmp(17 * 128 * 2, BF)
                bS = [Buf() for _ in range(17)]
                SBs = lambda n_: Tl(SBFr.ap[:, n_ * 128:(n_ + 1) * 128], [bS[n_]])
                NS32 = 4
                S32 = [tmp(512, F32) for _ in range(NS32)]
                OACC = tmp(S * 4, F32)
                kap = 128 ** -0.5
                items = []
                for hd in range(4):
                    for dr in range(2):
                        segs = list(range(NT)) if dr == 0 else list(range(NT - 1, -1, -1))
                        for n_, sg in enumerate(segs):
                            items.append((hd, dr, sg, n_ == 0, (dr == 1 and n_ == NT - 1)))
                hw = {}
                st = {"sidx": 0}

                def hgrn_weights(hd):
                    wa = wslot()
                    wqf = wview(wa, 8, 256)
                    wload(wa, wl, 8, C_RQ + hd * 128, 128, 0, 256)
                    wload(wa, wl, 8, C_RF + hd * 128, 128, 128, 256)
                    wb = wslot()
                    wbi = wview(wb, 8, 256)
                    wload(wb, wl, 8, C_RB + hd * 128, 128, 0, 256)
                    wload(wb, wl, 8, C_RI + hd * 128, 128, 128, 256)
                    wgg = wload(wslot(), wl, 8, C_RG + hd * 128, 128)
                    hw[hd] = (wqf, wbi, wgg)

                def front1(i):
                    hd, dr, sg, first, last = items[i]
                    if hd not in hw:
                        hgrn_weights(hd)
                    wqf, wbi, wgg = hw[hd]
                    d = DB[i % 2]
                    proj_fm(ps(0), wqf, 0, sg)
                    if dr == 0:
                        proj_fm(ps(1), wqf, 128, sg)
                    else:
                        proj_fm(ps(1), wbi, 0, sg)
                    for tb in range(4):
                        for kc in range(8):
                            k.mm(ps(2)[:, tb * 128:(tb + 1) * 128], Ht(kc, sg)[:, tb * 128:(tb + 1) * 128],
                                 wbi[:, kc, 128:256], start=(kc == 0), stop=(kc == 7))

                def front2(i):
                    hd, dr, sg, first, last = items[i]
                    d = DB[i % 2]
                    lbc = LBV[:, (dr * 4 + hd) * DEPTH + l:(dr * 4 + hd) * DEPTH + l + 1]
                    omc = OML[:, (dr * 4 + hd) * DEPTH + l:(dr * 4 + hd) * DEPTH + l + 1]
                    E1 = d["E1"]
                    lastoff = LCH - 1 if dr == 0 else 0
                    rv = lambda t: Tl(mkap(t, t.ap.offset + 511, [t.ap.ap[0], [-1, 512]]), t.bufs)
                    elast = Tl(mkap(E1, E1.ap.offset + lastoff, [E1.ap.ap[0], [LCH, 16], [0, LCH]]), E1.bufs)
                    ops = [
                        lambda: k.act(sgm, ps(1), AF.Sigmoid),
                        lambda: k.cp("dve", d["V"], Tl(ps(2).ap.rearrange("p (b e) -> p b e", e=128), ps(2).bufs)),
                        lambda: k.cp("act", d["A"], ps(0)),
                        lambda: k.ts("dve", sgm, sgm, omc, lbc, ALU.mult, ALU.add),
                        lambda: k.act(lgf, sgm, AF.Ln),
                        lambda: k.ts("pool", key, sgm, -1.0, 1.0, ALU.mult, ALU.add),
                        (lambda: k.scan(bb, SMKt, lgf, 0.0, ALU.mult, ALU.add)) if dr == 0 else
                        (lambda: k.scan(rv(bb), SMKt, rv(lgf), 0.0, ALU.mult, ALU.add)),
                        lambda: k.act(E1, bb, AF.Exp),
                        lambda: k.act(Einv, bb, AF.Exp, scale=-1.0),
                        lambda: k.stt(d["A"], d["A"], kap, E1, ALU.mult, ALU.mult),
                        lambda: k.tt("pool", Bbf, key, Einv, ALU.mult),
                        lambda: k.tt("pool", Tl(KdT.ap.rearrange("p (c t) -> p c t", t=LCH), KdT.bufs),
                                     Tl(Bbf.ap.rearrange("p (c t) -> p c t", t=LCH), Bbf.bufs), elast, ALU.mult),
                    ]
                    return ops

                def front3(i):
                    hd, dr, sg, first, last = items[i]
                    d = DB[i % 2]
                    for tb in range(4):
                        k.tr(psb(3)[:, tb * 128:(tb + 1) * 128], KdT[:, tb * 128:(tb + 1) * 128], IDBt)
                    k.cp("act", d["Kd"], Tl(psb(3).ap[:, 0:512].rearrange("p (b e) -> p b e", e=128), psb(3).bufs))
                    for tb in range(4):
                        k.mm(ps(4)[:, tb * 128:(tb + 1) * 128], Bbf[:, tb * 128:(tb + 1) * 128],
                             d["A"][:, tb * 128:(tb + 1) * 128])
                    trm = TRIt[:, dr * 128:(dr + 1) * 128]
                    k.tt("dve", d["ST"], Tl(ps(4).ap.rearrange("p (b t) -> p b t", t=128), ps(4).bufs),
                         Tl(mkap(trm, trm.ap.offset, [trm.ap.ap[0], [0, 4], [1, 128]]), trm.bufs), ALU.mult)

                ubank = [3, 4, 5, 6]

                def back1(i):
                    hd, dr, sg, first, last = items[i]
                    d = DB[i % 2]
                    Kdv, Vtv = d["Kd"], d["V"]
                    if first:
                        k.memset("pool", SBs(0), 0.0)
                        k.memset("dve", S32[0], 0.0)
                        st["sidx"] = 0
                    else:
                        k.cp("pool", SBs(0), SBs(16))
                    for tb in range(4):
                        for r in range(4):
                            k.mm(ps(ubank[r])[:, tb * 128:(tb + 1) * 128], Kdv[r * 32:(r + 1) * 32, tb, :],
                                 Vtv[r * 32:(r + 1) * 32, tb, :], tile_position=(r * 32, 0))

                def back2(i):
                    hd, dr, sg, first, last = items[i]
                    d = DB[i % 2]
                    E1 = d["E1"]
                    lastoff = LCH - 1 if dr == 0 else 0
                    order = list(range(16)) if dr == 0 else list(range(15, -1, -1))
                    slot_of = {}
                    ops = []
                    for n_, ch in enumerate(order):
                        slot_of[ch] = n_

                        def step(n_=n_, ch=ch):
                            tb, r = ch // 4, ch % 4
                            ecol = E1[:, ch * LCH + lastoff:ch * LCH + lastoff + 1]
                            sidx = st["sidx"]
                            k.stt(S32[(sidx + 1) % NS32], S32[sidx % NS32], ecol, ps(ubank[r])[:, tb * 128:(tb + 1) * 128],
                                  ALU.mult, ALU.add)
                            st["sidx"] = sidx + 1
                            k.cp("act" if n_ % 2 == 0 else "pool", SBs(n_ + 1), S32[(sidx + 1) % NS32])
                        ops.append(step)
                    return ops, slot_of

                def back3(i, slot_of):
                    hd, dr, sg, first, last = items[i]
                    wqf, wbi, wgg = hw[hd]
                    d = DB[i % 2]
                    Vtv, STv, Abf = d["V"], d["ST"], d["A"]
                    for tb in range(4):
                        k.mm(ps(7)[:, tb * 128:(tb + 1) * 128], Vtv[:, tb, :], STv[:, tb, :], start=True, stop=False)
                        for r in range(4):
                            ch = tb * 4 + r
                            k.mm(ps(7)[:, ch * LCH:(ch + 1) * LCH], SBs(slot_of[ch]),
                                 Abf[:, ch * LCH:(ch + 1) * LCH], start=False, stop=(r == 3))
                    oseg = OACC[:, sg * 512:(sg + 1) * 512]
                    if dr == 0:
                        k.cp("act", oseg, ps(7))
                    else:
                        k.tt("dve", oseg, oseg, ps(7), ALU.add)
                    if last:
                        SQh, RSh, SGh, TYh = sgm, lgf, xtra, bb
                        for tt in range(NT):
                            osg = OACC[:, tt * 512:(tt + 1) * 512]
                            sqb = Tl(SQh.ap.bitcast(BF)[:, 0:512], SQh.bufs)
                            k.act(sqb, osg, AF.Square)
                            k.mm(ps(7), ONBt, sqb)
                            k.act(RSh, ps(7), AF.Ln, bias=EPST, scale=1.0 / 128.0)
                            k.act(RSh, RSh, AF.Exp, scale=-0.5)
                            proj_fm(ps(3), wgg, 0, tt)
                            k.act(SGh, ps(3), AF.Silu)
                            k.stt(TYh, osg, HNG[:, l:l + 1], RSh, ALU.mult, ALU.mult)
                            k.tt("dve", Yt(8 + hd, tt), TYh, SGh, ALU.mult)

                nit_ = len(items)
                front1(0)
                for f_ in front2(0):
                    f_()
                if nit_ > 1:
                    front1(1)
                front3(0)
                for i in range(nit_):
                    fops, bops, slot_of = [], [], None
                    back1(i)
                    if i + 1 < nit_:
                        fops = front2(i + 1)
                    bops, slot_of = back2(i)
                    na, nb = len(fops), len(bops)
                    ia = ib = 0
                    did_f1 = False
                    while ia < na or ib < nb:
                        if ib >= nb or (ia < na and ia * max(nb, 1) <= ib * max(na, 1)):
                            fops[ia]()
                            ia += 1
                            if ia == 3 and i + 2 < nit_:
                                front1(i + 2)
                                did_f1 = True
                        else:
                            bops[ib]()
                            ib += 1
                    back3(i, slot_of)
                    if i + 1 < nit_:
                        front3(i + 1)

                if debug and l == layers[-1] and s == 0:
                    k.dma("sp", Tl(dbg_d, [Buf()]), Tl(Y[:, :], [b for row in bY for b in row]))

                for th in range(NHALF):
                    AR.reset()
                    Mh = tmp(NJ * HT * 2, BF)
                    Mv = Tl(Mh.ap.rearrange("p (j t) -> p j t", t=HT), Mh.bufs)
                    SG = [tmp(2048, F32), tmp(2048, F32)]
                    AC = [tmp(2048, F32), tmp(2048, F32)]
                    T2 = tmp(2048, F32)
                    for j in range(NJ):
                        for m in range(3):
                            wi = wslot()
                            wgv = wload(wi, wl, 8, C_GATE + m * D + j * 128, 128)
                            wuv = wload(wi, w_up_d[m][l], 4, j * 128, 128, off=1024)
                            for t2 in range(HNT):
                                tt = th * HNT + t2
                                ac = AC[t2 % 2]
                                gb, ub = (2 * (m * HNT + t2)) % 6, (2 * (m * HNT + t2) + 1) % 6
                                proj_fm(ps(gb), wgv, 0, tt)
                                sgt = SG[t2 % 2]
                                k.act(sgt, ps(gb), AF.Sigmoid)
                                for kc in range(4):
                                    k.mm(ps(ub), wuv[:, kc, 0:128], Yt(m * 4 + kc, tt),
                                         start=(kc == 0), stop=(kc == 3))
                                if m == 0:
                                    k.tt("dve", ac, sgt, ps(ub), ALU.mult)
                                elif m == 1:
                                    k.tt("dve", T2, sgt, ps(ub), ALU.mult)
                                    k.tt("dve", ac, ac, T2, ALU.add)
                                else:
                                    k.tt("dve", T2, sgt, ps(ub), ALU.mult)
                                    k.tt("dve", Mv[:, j, t2 * 512:(t2 + 1) * 512], ac, T2, ALU.add)
                                ndrain(1)
                    for j in range(NJ):
                        wo = wslot()
                        wov = wload(wo, w_out_d[l], 8, j * 128, 128)
                        for t2 in range(HNT):
                            tt = th * HNT + t2
                            pb = 6 + t2 % 2
                            proj_fm(ps(pb), wov, 0, tt, rhs_fn=lambda kc: Mv[:, kc, t2 * 512:(t2 + 1) * 512])
                            k.tt("dve", Xt(j, tt), Xt(j, tt), ps(pb), ALU.add)
                    assert AR.off <= NOFF
                    ndrain(len(npend))
                    npend.extend(rmsnorm_ops(G2, l, list(range(th * HNT, (th + 1) * HNT))))

                for th in range(NHALF):
                    AR.reset()
                    SL = [tmp(2048, F32), tmp(2048, F32)]
                    HIDv = Tl(Y[:, 0:NHC * HT].rearrange("p (c t) -> p c t", t=HT), [b for row in bY for b in row])
                    for hc in range(NHC):
                        wf = wslot()
                        wfv = wview(wf, 8, 256)
                        wload(wf, w_fi_d[l], 8, hc * 128, 128, 0, 256)
                        wload(wf, w_fi_d[l], 8, FFN + hc * 128, 128, 128, 256)
                        for t2 in range(HNT):
                            tt = th * HNT + t2
                            gb, ub = (2 * t2) % 4, (2 * t2 + 1) % 4
                            proj_fm(ps(gb), wfv, 0, tt)
                            proj_fm(ps(ub), wfv, 128, tt)
                            sl = SL[t2 % 2]
                            k.act(sl, ps(gb), AF.Silu)
                            k.tt("dve", HIDv[:, hc, t2 * 512:(t2 + 1) * 512], sl, ps(ub), ALU.mult)
                            ndrain(1)
                    ndrain(len(npend))
                    for j in range(NJ):
                        wo1 = wload(wslot(), w_fo_d[l], 11, j * 128, 128, k0=0)
                        wo2 = wload(wslot(), w_fo_d[l], 11, j * 128, 128, k0=11)
                        for t2 in range(HNT):
                            tt = th * HNT + t2
                            pb = 4 + (j * HNT + t2) % 4
                            for kc in range(NHC):
                                wsrc = wo1 if kc < 11 else wo2
                                k.mm(ps(pb), wsrc[:, kc % 11, 0:128], HIDv[:, kc, t2 * 512:(t2 + 1) * 512],
                                     start=(kc == 0), stop=(kc == NHC - 1))
                            k.tt("dve", Xt(j, tt), Xt(j, tt), ps(pb), ALU.add)
                    if th == 0 and NHALF == 2 and l != layers[-1]:
                        npend.extend(rmsnorm_ops(G1, l + 1, list(range(0, HNT))))

            AR.reset()
            if (final if final is not None else layers[-1] == DEPTH - 1):
                SQ = [tmp(1024, BF), tmp(1024, BF)]
                RS = [tmp(2048, F32), tmp(2048, F32)]
                for tt in range(NT):
                    pb = 6 + (tt % 2)
                    for j in range(NJ):
                        sq = SQ[j % 2]
                        k.act(sq, Xt(j, tt), AF.Square)
                        k.mm(ps(pb), ONBt, sq, start=(j == 0), stop=(j == NJ - 1))
                    rs = RS[tt % 2]
                    k.act(rs, ps(pb), AF.Ln, bias=EPST, scale=1.0 / D)
                    k.act(rs, rs, AF.Exp, scale=-0.5)
                    for j in range(NJ):
                        k.stt(Xt(j, tt), Xt(j, tt), GF[:, j:j + 1], rs, ALU.mult, ALU.mult)
            OS = [tmp(4096, F32), tmp(4096, F32)]
            n = 0
            for tt in range(NT):
                for tb in range(4):
                    os_ = OS[n % 2]
                    n += 1
                    pbs = (0, 1) if tb % 2 == 0 else (2, 3)
                    for j in range(NJ):
                        k.tr(ps(pbs[j // 4])[:, (j % 4) * 128:(j % 4 + 1) * 128], Xt(j, tt)[:, tb * 128:(tb + 1) * 128], IDFt)
                    k.cp("act", os_[:, 0:512], ps(pbs[0]))
                    k.cp("dve", os_[:, 512:1024], ps(pbs[1]))
                    t0 = tt * 512 + tb * 128
                    ob = Buf()
                    out_bufs.append(ob)
                    k.dma("sp", Tl(y_d[s, t0:t0 + 128, :], [ob]), os_)
        P.op("sp", None, reads=[Tl(None, out_bufs)], writes=[])

        with nc.Block() as block:
            engs = {}

            @block.tensor
            def _(e):
                engs["pe"] = e
                P_emit_one(P, "pe", e, sems, dma_sems)

            @block.vector
            def _(e):
                P_emit_one(P, "dve", e, sems, dma_sems)

            @block.scalar
            def _(e):
                P_emit_one(P, "act", e, sems, dma_sems)

            @block.gpsimd
            def _(e):
                P_emit_one(P, "pool", e, sems, dma_sems)

            @block.sync
            def _(e):
                P_emit_one(P, "sp", e, sems, dma_sems)
    return nc


_prepared = set()


def P_emit_one(P, ename, eng, sems, dma_sems):
    if id(P) not in _prepared:
        _prepared.add(id(P))
        for e in ENGS:
            for o in P.ops[e]:
                for d in o.deps:
                    d.need_inc = True
        for e in ENGS:
            cnt = 0
            kk = 0
            for o in P.ops[e]:
                if o.dma:
                    pool = dma_sems[e]
                    o.sem = pool[kk % len(pool)]
                    o.semval = 16 * (kk // len(pool) + 1)
                    kk += 1
                elif o.need_inc:
                    cnt += 1
                    o.sem = sems[e]
                    o.semval = cnt
    seen = {}
    for o in P.ops[ename]:
        waits = {}
        for d in o.deps:
            sid = id(d.sem)
            if sid not in waits or waits[sid][1] < d.semval:
                waits[sid] = (d.sem, d.semval)
        for sid, (s_, v) in waits.items():
            if seen.get(sid, 0) >= v:
                continue
            seen[sid] = v
            eng.wait_ge(s_, v)
        if o.fn is None:
            continue
        ins = o.fn(eng)
        if o.dma:
            ins.then_inc(o.sem, 16)
        elif o.need_inc:
            ins.then_inc(o.sem, 1)


def host_consts(S):
    NKC = S // 128
    identf = np.eye(128, dtype=np.float32)
    tri = np.zeros((128, 256), np.float32)
    for s_ in range(128):
        for t in range(128):
            if s_ // LCH == t // LCH:
                if s_ <= t:
                    tri[s_, t] = 1.0
                if s_ >= t:
                    tri[s_, 128 + t] = 1.0
    smask = np.ones((128, 512), np.float32)
    smask[:, ::LCH] = 0.0
    base = (NKC - 1) * 128
    TBW = S + base
    m = np.arange(TBW)[None, :]
    p = np.arange(128)[:, None]
    alibi = np.abs(p - m + base).astype(np.float32)
    return {"identf": identf, "tri": tri, "smask": smask, "alibi": alibi}


def host_params(inp):
    f = lambda a: np.ascontiguousarray(np.asarray(a, dtype=np.float32))
    L = DEPTH
    out = {}
    out["g1T"] = f(np.asarray(inp["norm1_g"]).reshape(L, 8, 128).transpose(2, 0, 1).reshape(128, L * 8))
    out["g2T"] = f(np.asarray(inp["norm2_g"]).reshape(L, 8, 128).transpose(2, 0, 1).reshape(128, L * 8))
    out["gfT"] = f(np.asarray(inp["final_norm_g"]).reshape(8, 128).T)
    out["pscT"] = f(np.asarray(inp["pool_scale"]).reshape(L, 4, 128).transpose(2, 0, 1).reshape(128, L * 4))
    out["dng"] = f(np.broadcast_to(np.asarray(inp["diff_norm_g"]).reshape(1, L * 128), (128, L * 128)))
    out["hngT"] = f(np.asarray(inp["hgrn_norm_g"]).T)
    lamv = np.concatenate([np.asarray(inp[n]).reshape(1, L * 64) for n in ("lam_q1", "lam_k1", "lam_q2", "lam_k2")], axis=1)
    out["lamv"] = f(np.broadcast_to(lamv, (128, 4 * L * 64)))
    out["lbT"] = f(np.asarray(inp["hgrn_lb"]).reshape(2, L, 4, 128).transpose(3, 0, 2, 1).reshape(128, 2 * 4 * L))
    for n in ("w_in", "pool_w", "w_up_pool", "w_up_attn", "w_up_rec", "w_out", "w_ffn_in", "w_ffn_out"):
        out[n] = f(inp[n])
    return out


_nc_cache = {}


def kernel(**inputs):
    x = np.asarray(inputs["x"], dtype=np.float32)
    B, S, _ = x.shape
    ncores = 8
    nseq = B // ncores
    shared = host_params(inputs)
    shared.update(host_consts(S))
    nc = build(S=S, NSEQ=nseq)
    in_maps = []
    for c in range(ncores):
        m = dict(shared)
        m["x"] = np.ascontiguousarray(x[c * nseq:(c + 1) * nseq])
        in_maps.append(m)
    res = run_bass_kernel_spmd(nc, in_maps, core_ids=list(range(ncores)))
    return np.concatenate([np.asarray(r["y"], dtype=np.float32) for r in res.results], axis=0)
```
